# Optimizing a Trainium2 kernel written in Bass

```python
import math
import jax, jax.numpy as jnp
from jax import lax
import numpy as np

D_MODEL = 1024
BATCH = 16
SEQ = 2048
DEPTH = 4

CHUNK = 64
Q_BLOCK = 128
N_HEADS = 8
HEAD_DIM = 128
D_ATT = N_HEADS * HEAD_DIM
D_RNN = D_MODEL
N_RNN_BLOCKS = 8
RNN_BLOCK = D_RNN // N_RNN_BLOCKS
CONV_WIDTH = 4
RG_C = 8.0
D_FF = -(-8 * D_MODEL // (3 * 256)) * 256
D_PLE = 256
DN_ALPHA = float((2 * DEPTH) ** 0.25)
DN_BETA = float((8 * DEPTH) ** -0.25)
LN_EPS = 1e-5

COLS = [D_ATT, D_ATT, D_ATT, N_HEADS, D_RNN, D_RNN, D_MODEL, D_MODEL]
N_IN = sum(COLS)
SPLITS = list(np.cumsum(COLS)[:-1].tolist())

kernel_name = "hybrid_fox_rglru_deepnorm_encoder"


def layer_norm(x, g, b):
    xf = x.astype(jnp.float32)
    mu = jnp.mean(xf, axis=-1, keepdims=True)
    var = jnp.mean(jnp.square(xf - mu), axis=-1, keepdims=True)
    y = (xf - mu) * lax.rsqrt(var + LN_EPS)
    return (y * g.astype(jnp.float32) + b.astype(jnp.float32)).astype(x.dtype)


def forgetting_attention(q, k, v, logf):
    S = q.shape[1]
    scale = 1.0 / math.sqrt(HEAD_DIM)
    c = jnp.cumsum(logf, axis=1).transpose(0, 2, 1)
    outs = []
    for blk in range(S // Q_BLOCK):
        t0, t1 = blk * Q_BLOCK, (blk + 1) * Q_BLOCK
        qb, kb, vb = q[:, t0:t1], k[:, :t1], v[:, :t1]
        s = jnp.einsum('bqhd,bkhd->bhqk', qb, kb).astype(jnp.float32) * scale
        s = s + c[:, :, t0:t1, None] - c[:, :, None, :t1]
        q_pos = t0 + jnp.arange(Q_BLOCK)[:, None]
        k_pos = jnp.arange(t1)[None, :]
        s = jnp.where(k_pos <= q_pos, s, -jnp.inf)
        pr = jax.nn.softmax(s, axis=-1)
        outs.append(jnp.einsum('bhqk,bkhd->bqhd', pr.astype(vb.dtype), vb))
    return jnp.concatenate(outs, axis=1)


def causal_depthwise_conv(x, w, b):
    y = lax.conv_general_dilated(
        x, w[:, None, :].astype(x.dtype), window_strides=(1,),
        padding=[(CONV_WIDTH - 1, 0)],
        dimension_numbers=('NWC', 'WIO', 'NWC'),
        feature_group_count=x.shape[-1])
    return y + b


def block_diag_linear(x, w, b):
    B, S, _ = x.shape
    xr = x.reshape(B, S, N_RNN_BLOCKS, RNN_BLOCK)
    return jnp.einsum('bsnc,ncd->bsnd', xr, w).reshape(B, S, D_RNN) + b


def _lin_rec_combine(e1, e2):
    a1, b1 = e1
    a2, b2 = e2
    return a1 * a2, a2 * b1 + b2


def rg_lru_branch(rx, ry, conv_w, conv_b, w_a, b_a, w_x, b_x, lam):
    xc = causal_depthwise_conv(rx, conv_w, conv_b)
    r = jax.nn.sigmoid(block_diag_linear(xc, w_a, b_a).astype(jnp.float32))
    i = jax.nn.sigmoid(block_diag_linear(xc, w_x, b_x).astype(jnp.float32))
    log_a = -RG_C * jax.nn.softplus(-lam.astype(jnp.float32)) * r
    a = jnp.exp(log_a)
    mult = jnp.sqrt(-jnp.expm1(2.0 * log_a))
    u = mult * (i * xc.astype(jnp.float32))
    _, h = lax.associative_scan(_lin_rec_combine, (a, u), axis=1)
    return h.astype(rx.dtype) * jax.nn.gelu(ry)


def hybrid_mixer(u, w_in, b_forget, conv_w, conv_b, w_a, b_a, w_x, b_x, lam,
                 w_br_att, w_br_rnn, b_merge, w_out):
    B, S, _ = u.shape
    z = u @ w_in
    q, k, v, f_logit, rx, ry, ga, gb = jnp.split(z, SPLITS, axis=-1)
    q = q.reshape(B, S, N_HEADS, HEAD_DIM)
    k = k.reshape(B, S, N_HEADS, HEAD_DIM)
    v = v.reshape(B, S, N_HEADS, HEAD_DIM)
    logf = jax.nn.log_sigmoid((f_logit + b_forget).astype(jnp.float32))
    att = forgetting_attention(q, k, v, logf).reshape(B, S, D_ATT)
    rnn = rg_lru_branch(rx, ry, conv_w, conv_b, w_a, b_a, w_x, b_x, lam)
    ya = att @ w_br_att
    yb = rnn @ w_br_rnn
    merged = jax.nn.sigmoid(ga + b_merge[0]) * ya + jax.nn.sigmoid(gb + b_merge[1]) * yb
    return merged @ w_out


def swiglu(x, w_in, w_out):
    hg, hu = jnp.split(x @ w_in, 2, axis=-1)
    return (jax.nn.silu(hg) * hu) @ w_out


def setup_inputs(seed: int = 0) -> dict:
    key = jax.random.key(seed)
    ks = jax.random.split(key, 32)
    f32 = jnp.float32
    L, D = DEPTH, D_MODEL

    def nrm(k, shape, scale):
        return jax.random.normal(k, shape, f32) * scale

    u_lam = jax.random.uniform(ks[10], (L, D_RNN), f32, 0.9, 0.999)
    a0 = u_lam ** (1.0 / RG_C)
    rg_lambda = jnp.log(a0) - jnp.log1p(-a0)

    return {
        "x": nrm(ks[0], (BATCH, SEQ, D), 1.0),
        "p": nrm(ks[1], (DEPTH, BATCH, SEQ, D_PLE), 1.0),
        "ln_in_g": 1.0 + nrm(ks[2], (D,), 0.02),
        "ln_in_b": nrm(ks[3], (D,), 0.02),
        "w_in": nrm(ks[4], (L, D, N_IN), D ** -0.5),
        "b_forget": jax.random.uniform(ks[5], (L, N_HEADS), f32, 1.0, 6.0),
        "conv_w": nrm(ks[6], (L, CONV_WIDTH, D_RNN), CONV_WIDTH ** -0.5),
        "conv_b": nrm(ks[7], (L, D_RNN), 0.02),
        "rg_w_a": nrm(ks[8], (L, N_RNN_BLOCKS, RNN_BLOCK, RNN_BLOCK), RNN_BLOCK ** -0.5),
        "rg_b_a": nrm(ks[9], (L, D_RNN), 0.02),
        "rg_w_x": nrm(ks[11], (L, N_RNN_BLOCKS, RNN_BLOCK, RNN_BLOCK), RNN_BLOCK ** -0.5),
        "rg_b_x": nrm(ks[12], (L, D_RNN), 0.02),
        "rg_lambda": rg_lambda,
        "w_branch_att": nrm(ks[13], (L, D_ATT, D), D_ATT ** -0.5),
        "w_branch_rnn": nrm(ks[14], (L, D_RNN, D), D_RNN ** -0.5),
        "b_merge": nrm(ks[15], (L, 2, D), 0.02),
        "w_out": nrm(ks[16], (L, D, D), D ** -0.5 * DN_BETA),
        "ln_mix_g": 1.0 + nrm(ks[17], (L, D), 0.02),
        "ln_mix_b": nrm(ks[18], (L, D), 0.02),
        "w_ffn_in": nrm(ks[19], (L, D, 2 * D_FF), D ** -0.5),
        "w_ffn_out": nrm(ks[20], (L, D_FF, D), D_FF ** -0.5 * DN_BETA),
        "ln_ffn_g": 1.0 + nrm(ks[21], (L, D), 0.02),
        "ln_ffn_b": nrm(ks[22], (L, D), 0.02),
        "w_ple": nrm(ks[23], (L, D_PLE, D), D_PLE ** -0.5 * DN_BETA),
        "w_ple_gate": nrm(ks[24], (L, D, D), D ** -0.5),
        "b_ple_gate": nrm(ks[25], (L, D), 0.02),
        "ln_ple_g": 1.0 + nrm(ks[26], (L, D), 0.02),
        "ln_ple_b": nrm(ks[27], (L, D), 0.02),
    }


def reference(x, p, ln_in_g, ln_in_b, w_in, b_forget, conv_w, conv_b, rg_w_a, rg_b_a,
              rg_w_x, rg_b_x, rg_lambda, w_branch_att, w_branch_rnn, b_merge, w_out,
              ln_mix_g, ln_mix_b, w_ffn_in, w_ffn_out, ln_ffn_g, ln_ffn_b,
              w_ple, w_ple_gate, b_ple_gate, ln_ple_g, ln_ple_b):
    h = layer_norm(x, ln_in_g, ln_in_b)
    for l in range(DEPTH):
        m = hybrid_mixer(h, w_in[l], b_forget[l], conv_w[l], conv_b[l],
                         rg_w_a[l], rg_b_a[l], rg_w_x[l], rg_b_x[l], rg_lambda[l],
                         w_branch_att[l], w_branch_rnn[l], b_merge[l], w_out[l])
        h = layer_norm(DN_ALPHA * h + m, ln_mix_g[l], ln_mix_b[l])
        f = swiglu(h, w_ffn_in[l], w_ffn_out[l])
        h = layer_norm(DN_ALPHA * h + f, ln_ffn_g[l], ln_ffn_b[l])
        e = jax.nn.sigmoid(h @ w_ple_gate[l] + b_ple_gate[l]) * (p[l] @ w_ple[l])
        h = layer_norm(DN_ALPHA * h + e, ln_ple_g[l], ln_ple_b[l])
    return h
```

```python
import math
import numpy as np
import concourse.bass as bass
import concourse.mybir as mybir
from concourse.bass_utils import run_bass_kernel_spmd

F32 = mybir.dt.float32
BF16 = mybir.dt.bfloat16
U8 = mybir.dt.uint8
AF = mybir.ActivationFunctionType
ALU = mybir.AluOpType

L = 4
D = 1024
T = 2048
KC = 8
TT = 512
NTT = 4
NH = 8
NCORES = 8
ALPHA = float((2 * L) ** 0.25)
LN_EPS = 1e-5
D_FF = 2816
NFF = 22
BLK = 512
WINDOWS = {"pe": 48, "act": 48, "dve": 48, "pool": 48, "sp": 48}
REORDER = True

CPL = 136
C_MIXG, C_MIXB, C_FFNG, C_FFNB, C_PLEG, C_PLEB = 0, 8, 16, 24, 32, 40
C_CONVW, C_CONVB, C_BA, C_BX, C_LAM, C_BM0, C_BM1, C_BPG = 48, 80, 88, 96, 104, 112, 120, 128
C_ING = L * CPL
C_INB = C_ING + 8
C_BF = C_INB + 8
NC = C_BF + L * 8

SL_RXY, SL_GATES, SL_V, SL_QK, SL_WF = 0, 8, 9, 13, 21
SL_GA, SL_GB, SL_BRA, SL_BRB, SL_WOUT = 22, 26, 30, 34, 38
SL_HG, SL_HU, SL_FFO, SL_WPG, SL_WPLE = 42, 53, 64, 80, 84
NSLAB = 85
SLAB_ELEMS = 2048
NSLOT = 6


def _slabk(m):
    k = m.shape[0] // 128
    n = m.shape[1]
    return np.ascontiguousarray(m.reshape(k, 128, n).transpose(1, 0, 2)).reshape(128, k * n)


def _build_slabs(inp):
    W = np.zeros((L, NSLAB, 128, SLAB_ELEMS), np.float32)
    for l in range(L):
        w_in = inp["w_in"][l]

        def put(si, arr):
            W[l, si, :, : arr.shape[1]] = arr

        for c in range(8):
            put(SL_RXY + c, _slabk(np.concatenate(
                [w_in[:, 3080 + c * 128: 3080 + (c + 1) * 128], w_in[:, 4104 + c * 128: 4104 + (c + 1) * 128]], axis=1)))
        ga = np.concatenate([inp["rg_w_a"][l], inp["rg_w_x"][l]], axis=2)
        put(SL_GATES, np.ascontiguousarray(ga.transpose(1, 0, 2)).reshape(128, 8 * 256))
        for g in range(4):
            put(SL_V + g, _slabk(w_in[:, 2048 + g * 256: 2048 + (g + 1) * 256]))
        for hd in range(8):
            put(SL_QK + hd, _slabk(np.concatenate(
                [w_in[:, hd * 128:(hd + 1) * 128], w_in[:, 1024 + hd * 128: 1024 + (hd + 1) * 128]], axis=1)))
        put(SL_WF, _slabk(w_in[:, 3072:3080]))
        for s in range(4):
            put(SL_GA + s, _slabk(w_in[:, 5128 + s * 256: 5128 + (s + 1) * 256]))
            put(SL_GB + s, _slabk(w_in[:, 6152 + s * 256: 6152 + (s + 1) * 256]))
            put(SL_BRA + s, _slabk(inp["w_branch_att"][l][:, s * 256:(s + 1) * 256]))
            put(SL_BRB + s, _slabk(inp["w_branch_rnn"][l][:, s * 256:(s + 1) * 256]))
            put(SL_WOUT + s, _slabk(inp["w_out"][l][:, s * 256:(s + 1) * 256]))
            put(SL_WPG + s, _slabk(inp["w_ple_gate"][l][:, s * 256:(s + 1) * 256]))
        wfi = inp["w_ffn_in"][l]
        for s in range(11):
            put(SL_HG + s, _slabk(wfi[:, s * 256:(s + 1) * 256]))
            put(SL_HU + s, _slabk(wfi[:, D_FF + s * 256: D_FF + (s + 1) * 256]))
        wfo = inp["w_ffn_out"][l]
        for c in range(8):
            for h in range(2):
                put(SL_FFO + 2 * c + h, _slabk(wfo[h * 1408:(h + 1) * 1408, c * 128:(c + 1) * 128]))
        put(SL_WPLE, _slabk(inp["w_ple"][l]))
    return W


def _fm(v):
    return np.ascontiguousarray(np.asarray(v, np.float32).reshape(8, 128).T)


def _build_consts(inp):
    C = np.zeros((128, NC), np.float32)
    for l in range(L):
        b = l * CPL
        C[:, b + C_MIXG: b + C_MIXG + 8] = _fm(inp["ln_mix_g"][l])
        C[:, b + C_MIXB: b + C_MIXB + 8] = _fm(inp["ln_mix_b"][l])
        C[:, b + C_FFNG: b + C_FFNG + 8] = _fm(inp["ln_ffn_g"][l])
        C[:, b + C_FFNB: b + C_FFNB + 8] = _fm(inp["ln_ffn_b"][l])
        C[:, b + C_PLEG: b + C_PLEG + 8] = _fm(inp["ln_ple_g"][l])
        C[:, b + C_PLEB: b + C_PLEB + 8] = _fm(inp["ln_ple_b"][l])
        for k in range(4):
            C[:, b + C_CONVW + 8 * k: b + C_CONVW + 8 * k + 8] = _fm(inp["conv_w"][l][k])
        C[:, b + C_CONVB: b + C_CONVB + 8] = _fm(inp["conv_b"][l])
        C[:, b + C_BA: b + C_BA + 8] = _fm(inp["rg_b_a"][l])
        C[:, b + C_BX: b + C_BX + 8] = _fm(inp["rg_b_x"][l])
        C[:, b + C_LAM: b + C_LAM + 8] = _fm(inp["rg_lambda"][l])
        C[:, b + C_BM0: b + C_BM0 + 8] = _fm(inp["b_merge"][l][0])
        C[:, b + C_BM1: b + C_BM1 + 8] = _fm(inp["b_merge"][l][1])
        C[:, b + C_BPG: b + C_BPG + 8] = _fm(inp["b_ple_gate"][l])
        C[:, C_BF + 8 * l: C_BF + 8 * l + 8] = np.broadcast_to(np.asarray(inp["b_forget"][l], np.float32)[None, :], (128, 8))
    C[:, C_ING: C_ING + 8] = _fm(inp["ln_in_g"])
    C[:, C_INB: C_INB + 8] = _fm(inp["ln_in_b"])
    return C


def _build_masks():
    r = np.arange(128)[:, None]
    c = np.arange(128)[None, :]
    ident = (r == c).astype(np.float32)
    maskneg = np.where(c < r, -30000.0, 0.0).astype(np.float32)
    ones = np.ones((128, 128), np.float32)
    tri = (r <= c).astype(np.float32)
    half = np.broadcast_to((r <= 63).astype(np.float32), (128, 128))
    return np.ascontiguousarray(np.concatenate([ident, maskneg, ones, tri, half], axis=1)), np.ones((128, 16), np.float32)


class _Op:
    __slots__ = ("idx", "eng", "fn", "deps", "dma", "needed", "token")

    def __init__(self, idx, eng, fn, deps, dma):
        self.idx, self.eng, self.fn, self.deps, self.dma = idx, eng, fn, deps, dma
        self.needed = False
        self.token = None


_ACT_SET = {AF.Sigmoid: "sig", AF.Exp: "exp", AF.Ln: "exp", AF.Gelu_apprx_tanh: "gelu", AF.Silu: "silu"}
ACT_SWITCH_NS = 1300.0


class _Probe:
    def __init__(self, eng):
        self.eng = eng
        self.cost = 0.0
        self.xfer = 0.0
        self.aset = None

    def then_inc(self, *a, **k):
        return self

    def matmul(self, out, lhsT, rhs, **kw):
        n = int(np.prod(rhs.shape[1:]))
        mult = 4.0 if rhs.tensor.dtype == F32 else 1.0
        self.cost += (max(n, 64) * 0.45 + 12.0) * mult
        return self

    def dma_start(self, out, in_, **kw):
        nbytes = float(np.prod(in_.shape)) * 4.0
        self.cost += 900.0 if self.eng == "pool" else 150.0
        self.xfer += 2200.0 + nbytes / 220.0
        return self

    def nop(self, *a, **k):
        self.cost += 50.0
        return self

    def _ew(self, out, scan=False):
        n = float(np.prod(out.shape[1:]))
        if self.eng == "act":
            self.cost += 230.0 + 0.75 * n
        elif self.eng == "dve":
            self.cost += 110.0 + (2.1 if scan else 1.05) * n
        else:
            self.cost += 220.0 + 1.1 * n
        return self

    def activation(self, out, in_, func, **kw):
        st = _ACT_SET.get(func)
        if st is not None:
            self.aset = st
        return self._ew(out)

    def tensor_tensor_scan(self, out, **kw):
        return self._ew(out, scan=True)

    def tensor_tensor(self, out, **kw):
        return self._ew(out)

    def tensor_scalar(self, out, **kw):
        return self._ew(out)

    def scalar_tensor_tensor(self, out, **kw):
        return self._ew(out)

    def tensor_copy(self, out, **kw):
        return self._ew(out)

    def reciprocal(self, out, **kw):
        return self._ew(out)

    def memset(self, ap, *a, **kw):
        return self._ew(ap)


class Sched:
    ENGS = ("pe", "act", "dve", "pool", "sp")

    def __init__(self):
        self.ops = []
        self.last_w = {}
        self.readers = {}

    def add(self, eng, fn, reads=(), writes=(), dma=None):
        idx = len(self.ops)
        deps = {}
        for r in reads:
            w = self.last_w.get(r)
            if w is not None:
                deps[w] = True
        for k in writes:
            w = self.last_w.get(k)
            if w is not None and w not in deps:
                deps[w] = False
            for rd in self.readers.get(k, ()):
                if rd not in deps:
                    deps[rd] = False
        deps.pop(idx, None)
        for r in reads:
            self.readers.setdefault(r, []).append(idx)
        for k in writes:
            self.last_w[k] = idx
            self.readers[k] = []
        self.ops.append(_Op(idx, eng, fn, deps, dma))
        return idx

    def _list_schedule(self, lat=180.0):
        ops = self.ops
        n = len(ops)
        cost = [0.0] * n
        xfer = [0.0] * n
        aset = [None] * n
        cur_set = [None]
        for op in ops:
            pr = _Probe(op.eng)
            op.fn(pr)
            cost[op.idx] = pr.cost
            xfer[op.idx] = pr.xfer
            aset[op.idx] = pr.aset
        succ = [[] for _ in range(n)]
        nleft = [0] * n
        for op in ops:
            nleft[op.idx] = len(op.deps)
            for d in op.deps:
                succ[d].append(op.idx)
        by_eng = {e: [op.idx for op in ops if op.eng == e] for e in self.ENGS}
        ptr = {e: 0 for e in self.ENGS}
        done = [False] * n
        ready = [0.0] * n
        finish = [0.0] * n
        efree = {e: 0.0 for e in self.ENGS}
        order = {e: [] for e in self.ENGS}
        remaining = n
        while remaining:
            best = None
            for e in self.ENGS:
                lst = by_eng[e]
                i = ptr[e]
                while i < len(lst) and done[lst[i]]:
                    i += 1
                ptr[e] = i
                cnt = 0
                j = i
                ef = efree[e]
                while j < len(lst) and cnt < WINDOWS[e]:
                    k = lst[j]
                    j += 1
                    if done[k]:
                        continue
                    cnt += 1
                    if nleft[k]:
                        continue
                    st = ready[k] if ready[k] > ef else ef
                    if e == "act":
                        sw = aset[k] is not None and aset[k] != cur_set[0]
                        sc = st + (ACT_SWITCH_NS if sw else 0.0)
                        if best is None or sc < best[0]:
                            best = (sc, e, k, st, sw)
                        if sc <= ef:
                            break
                    else:
                        if best is None or st < best[0]:
                            best = (st, e, k, st, False)
                        if st <= ef:
                            break
            _sc, e, k, st, sw = best
            done[k] = True
            remaining -= 1
            if sw:
                st += ACT_SWITCH_NS
            if e == "act" and aset[k] is not None:
                cur_set[0] = aset[k]
            efree[e] = st + cost[k]
            finish[k] = st + cost[k] + xfer[k]
            order[e].append(k)
            for sidx in succ[k]:
                nleft[sidx] -= 1
                so = ops[sidx]
                same = (so.eng == e) and (ops[k].dma is None) and (so.dma is None)
                f = finish[k] + (0.0 if same else lat)
                if f > ready[sidx]:
                    ready[sidx] = f
        self.sim_time = max(finish) if n else 0.0
        return order

    def emit(self, nc, reorder=True):
        ops = self.ops
        if reorder:
            order = self._list_schedule()
        else:
            order = {e: [op.idx for op in ops if op.eng == e] for e in self.ENGS}
        for op in ops:
            keep = {}
            for d, raw in op.deps.items():
                p = ops[d]
                same = (p.eng == op.eng) and (p.dma is None) and (op.dma is None)
                if same and (op.eng == "pe" or not raw):
                    continue
                keep[d] = raw
            op.deps = keep
            for d in keep:
                ops[d].needed = True
        engs = list(self.ENGS)
        dma_keys = sorted({op.dma for op in ops if op.dma is not None}, key=str)
        sem_names = ["e_" + e for e in engs[:4]] + ["d_%d" % i for i in range(len(dma_keys))]
        from contextlib import ExitStack
        with ExitStack() as st:
            sems = [st.enter_context(nc.semaphore(n)) for n in sem_names]
            esem = {e: sems[i] for i, e in enumerate(engs[:4])}
            dsem = {k: sems[4 + i] for i, k in enumerate(dma_keys)}
            cnt = {}
            for e in engs:
                for k in order[e]:
                    op = ops[k]
                    if op.dma is not None:
                        cnt[op.dma] = cnt.get(op.dma, 0) + 16
                        op.token = (dsem[op.dma], cnt[op.dma])
                    elif op.needed:
                        cnt[op.eng] = cnt.get(op.eng, 0) + 1
                        op.token = (esem[op.eng], cnt[op.eng])
            block = st.enter_context(nc.Block())

            def run(eng_name):
                def body(e):
                    waited = {}
                    for k in order[eng_name]:
                        op = ops[k]
                        for d in op.deps:
                            s, v = ops[d].token
                            key = id(s)
                            if waited.get(key, 0) < v:
                                e.wait_ge(s, v)
                                waited[key] = v
                        ins = op.fn(e)
                        if op.dma is not None:
                            ins.then_inc(op.token[0], 16)
                        elif op.needed:
                            ins.then_inc(op.token[0], 1)
                return body

            block.tensor(run("pe"))
            block.scalar(run("act"))
            block.vector(run("dve"))
            block.gpsimd(run("pool"))
            block.sync(run("sp"))


class _Pool:
    def __init__(self, items):
        self.free_list = list(items)

    def alloc(self):
        return self.free_list.pop(0)

    def free(self, *its):
        for it in its:
            self.free_list.append(it)


class Buf:
    def __init__(self, nc, name, base, off, shape, dtype):
        self.t = nc.alloc_sbuf_tensor_at(name, shape, dtype, offset=base + off)
        self.off = off
        self.esz = 2 if dtype == BF16 else 4
        self.row = int(np.prod(shape[1:]))
        self.inner = shape[-1]
        self.nbytes = self.row * self.esz

    def keys(self, lo=0, hi=None):
        if hi is None:
            hi = self.row
        a = self.off + lo * self.esz
        b = self.off + hi * self.esz
        return [("sb", i) for i in range(a // BLK, (b + BLK - 1) // BLK)]

    def keys3(self, chunks, lo, hi):
        out = []
        for c in chunks:
            out += self.keys(c * self.inner + lo, c * self.inner + hi)
        return out


class PsBank:
    def __init__(self, nc, i):
        self.t = nc.alloc_psum_tensor("ps%d" % i, [128, 512], F32)
        self.key = ("ps", i)


def build_program(n_layers=L, n_seq=2, dbg=False):
    nc = bass.Bass("TRN2", target_bir_lowering=False)
    xT = nc.dram_tensor("xT", [n_seq, D, T], F32, kind="ExternalInput").ap()
    pT = nc.dram_tensor("pT", [L, n_seq, 256, T], F32, kind="ExternalInput").ap()
    Wd = nc.dram_tensor("W", [L, NSLAB, 128, SLAB_ELEMS], F32, kind="ExternalInput").ap()
    cstd = nc.dram_tensor("cst", [128, NC], F32, kind="ExternalInput").ap()
    m16d = nc.dram_tensor("m16", [128, 640], F32, kind="ExternalInput").ap()
    m32d = nc.dram_tensor("m32", [128, 16], F32, kind="ExternalInput").ap()
    outT = nc.dram_tensor("outT", [n_seq, D, T], F32, kind="ExternalOutput").ap()

    ARENA = 212480
    arena = nc.alloc_sbuf_tensor("arena", [128, ARENA], U8)
    base = nc.lookup_mloc(arena).addr

    def mk(name, off, shape, dtype):
        return Buf(nc, name, base, off, shape, dtype)

    O_HI, O_LO, O_A, O_B, O_W1, O_WS, O_PSL, O_TMP, O_MISC = 0, 32768, 65536, 98304, 131072, 163840, 188416, 192512, 200704
    HI = mk("hi", O_HI, [128, 8, T], BF16)
    LO = mk("lo", O_LO, [128, 8, T], BF16)
    ATT = mk("attT", O_A, [128, 8, T], BF16)
    RNN = mk("rnnT", O_B, [128, 8, T], BF16)
    RXP = mk("rxp", O_A, [128, T + 4], F32)
    XC = [mk("xc%d" % i, O_A + 8704 + 4096 * i, [128, 1024], F32) for i in range(2)]
    RR = [mk("rr%d" % i, O_A + 16896 + 4096 * i, [128, 1024], F32) for i in range(2)]
    XCB = [mk("xcb%d" % i, O_A + 25088 + 2048 * i, [128, 1024], BF16) for i in range(2)]
    II = [mk("ii%d" % i, O_W1 + 4096 * i, [128, 1024], F32) for i in range(2)]
    AA = [mk("aa%d" % i, O_W1 + 8192 + 4096 * i, [128, 1024], F32) for i in range(2)]
    HS = [mk("hs%d" % i, O_W1 + 16384 + 4096 * i, [128, 1024], F32) for i in range(2)]
    VB = [mk("vb%d" % i, O_W1 + 8192 * i, [128, 16, 256], BF16) for i in range(2)]
    QT = [mk("qt%d" % i, O_W1 + 16384 + 4096 * i, [128, T], BF16) for i in range(2)]
    KT = [mk("kt%d" % i, O_W1 + 24576 + 4096 * i, [128, T], BF16) for i in range(2)]
    MG = mk("mg", O_W1, [128, 8, T], BF16)
    X1W = [mk("x1w%d" % i, O_W1 + 16384 * i, [128, 8, TT], F32) for i in range(2)]
    X1A = [mk("x1a%d" % i, O_A + 16384 * i, [128, 8, TT], F32) for i in range(2)]
    ACTT = mk("actt", O_A, [128, NFF, 1024], BF16)
    PTB = mk("ptb", O_A, [128, 2, T], BF16)
    LNSQ = mk("lnsq", O_B + 16384, [128, 8, TT], BF16)
    LNXB = mk("lnxb", O_B + 24576, [128, 8, TT], BF16)
    WS = [mk("ws%d" % i, O_WS + 4096 * i, [128, SLAB_ELEMS], BF16) for i in range(NSLOT)]
    PSL = [mk("psl%d" % i, O_PSL + 1024 * i, [128, TT], BF16) for i in range(4)]
    TMPS = [mk("tmp%d" % i, O_TMP + 2048 * i, [128, TT], F32) for i in range(4)]
    CST = mk("cstb", O_MISC, [128, NC], F32)
    DER = mk("der", O_MISC + 2560, [128, L * 16], F32)
    M16 = mk("m16b", O_MISC + 3072, [128, 640], BF16)
    SP3 = [mk("sp3_%d" % i, O_MISC + 4352 + 256 * i, [128, 128], BF16) for i in range(3)]
    R1B = mk("r1b", O_MISC + 10240, [128, 128], F32)
    M32 = mk("m32b", O_MISC + 2816, [128, 16], F32)
    XF = mk("xf", O_MISC + 5120, [128, 16, 8], F32)
    SPB = mk("spb", O_MISC + 5632, [128, 16, 8], F32)
    SBK = mk("sbk", O_MISC + 6144, [128, 16, 8], F32)
    INC = mk("inc", O_MISC + 6656, [128, 16, 8], F32)
    EXB = mk("exb", O_MISC + 7168, [128, 16, 8], F32)
    CSB = mk("csb", O_MISC + 7680, [128, 16, 8], F32)
    BT = [mk("bt%d" % i, O_MISC + 8192 + 1024 * i, [128, 16, 16], F32) for i in range(2)]

    PSB = [PsBank(nc, i) for i in range(8)]
    PS = _Pool(PSB)
    TMP = _Pool(TMPS)
    S = Sched()

    ident_bf = M16.t[:, 0:128]
    maskneg_bf = M16.t[:, 128:256]
    ones_bf = M16.t[:, 256:384]
    tri_bf = M16.t[:, 384:512]
    half_bf = M16.t[:, 512:640]
    ones32 = M32.t[:, 0:16]
    KM16 = M16.keys()
    KM32 = M32.keys()
    KCST = CST.keys()

    def cc(col):
        return CST.t[:, col:col + 1]

    def slab_sequence():
        seq = []
        for _s in range(n_seq):
            for l in range(n_layers):
                seq.append((l, SL_GATES, 2048))
                for c in range(8):
                    seq.append((l, SL_RXY + c, 2048))
                seq.append((l, SL_WF, 64))
                for hd in range(8):
                    if hd % 2 == 0:
                        seq.append((l, SL_V + hd // 2, 2048))
                    seq.append((l, SL_QK + hd, 2048))
                for s in range(4):
                    for b in (SL_GA, SL_GB, SL_BRA, SL_BRB):
                        seq.append((l, b + s, 2048))
                for s in range(4):
                    seq.append((l, SL_WOUT + s, 2048))
                for _h in range(2):
                    for s in range(11):
                        seq.append((l, SL_HG + s, 2048))
                        seq.append((l, SL_HU + s, 2048))
                    for c in range(8):
                        seq.append((l, SL_FFO + 2 * c, 1408))
                        seq.append((l, SL_FFO + 2 * c + 1, 1408))
                for s in range(4):
                    seq.append((l, SL_WPG + s, 2048))
                seq.append((l, SL_WPLE, 2048))
        return seq

    class WQ:
        def __init__(self):
            self.seq = slab_sequence()
            self.nxt = 0
            self.cons = 0
            self.slot_of = {}
            for i in range(NSLOT):
                self._load(i)

        def _load(self, slot):
            if self.nxt >= len(self.seq):
                return
            l, si, n = self.seq[self.nxt]
            self.slot_of[self.nxt] = slot
            self.nxt += 1
            ws = WS[slot]
            S.add("pool", lambda e, ws=ws, l=l, si=si, n=n: e.dma_start(out=ws.t[:, 0:n], in_=Wd[l, si, :, 0:n]),
                  writes=ws.keys(), dma=("ws", slot))

        def get(self, l, si):
            assert self.seq[self.cons][:2] == (l, si), (self.seq[self.cons], l, si)
            slot = self.slot_of.pop(self.cons)
            self.cons += 1
            return slot

        def done(self, *slots):
            for s in slots:
                self._load(s)

    def mm_group(out_ap, pairs):
        def fn(e, out_ap=out_ap, pairs=pairs):
            n = len(pairs)
            ins = None
            for i, (lt, rh) in enumerate(pairs):
                ins = e.matmul(out_ap, lhsT=lt, rhs=rh, start=(i == 0), stop=(i == n - 1))
            return ins
        return fn

    def hik(tt):
        return HI.keys3(range(8), tt * TT, (tt + 1) * TT)

    def lok(tt):
        return LO.keys3(range(8), tt * TT, (tt + 1) * TT)

    def layernorm(xb, gcol, bcol, tt, final=False, seq=0, eps=LN_EPS):
        xk = xb.keys()
        S.add("act", lambda e: e.activation(out=LNSQ.t[:, :, :], in_=xb.t[:, :, :], func=AF.Square),
              reads=xk, writes=LNSQ.keys())
        S.add("act", lambda e: e.activation(out=LNXB.t[:, :, :], in_=xb.t[:, :, :], func=AF.Copy),
              reads=xk, writes=LNXB.keys())
        p1 = PS.alloc()
        p2 = PS.alloc()
        S.add("pe", mm_group(p1.t[:, :], [(ones_bf, LNXB.t[:, kc, :]) for kc in range(8)]),
              reads=LNXB.keys() + KM16, writes=[p1.key])
        S.add("pe", mm_group(p2.t[:, :], [(ones_bf, LNSQ.t[:, kc, :]) for kc in range(8)]),
              reads=LNSQ.keys() + KM16, writes=[p2.key])
        tA = TMP.alloc()
        tB = TMP.alloc()
        S.add("act", lambda e: e.activation(out=tA.t[:, :], in_=p1.t[:, :], func=AF.Copy, scale=1.0 / D),
              reads=[p1.key], writes=tA.keys())
        S.add("act", lambda e: e.activation(out=tB.t[:, :], in_=p1.t[:, :], func=AF.Square, scale=1.0 / D),
              reads=[p1.key], writes=tB.keys())
        S.add("dve", lambda e: e.scalar_tensor_tensor(out=tB.t[:, :], in0=p2.t[:, :], scalar=1.0 / D, in1=tB.t[:, :],
                                                     op0=ALU.mult, op1=ALU.subtract),
              reads=[p2.key] + tB.keys(), writes=tB.keys())
        PS.free(p1, p2)
        S.add("dve", lambda e: e.tensor_scalar(out=tB.t[:, :], in0=tB.t[:, :], scalar1=float(eps), scalar2=None, op0=ALU.add),
              reads=tB.keys(), writes=tB.keys())
        S.add("act", lambda e: e.activation(out=tB.t[:, :], in_=tB.t[:, :], func=AF.Ln), reads=tB.keys(), writes=tB.keys())
        S.add("act", lambda e: e.activation(out=tB.t[:, :], in_=tB.t[:, :], func=AF.Exp, scale=-0.5),
              reads=tB.keys(), writes=tB.keys())
        S.add("dve", lambda e: e.scalar_tensor_tensor(out=tA.t[:, :], in0=tA.t[:, :], scalar=-1.0, in1=tB.t[:, :],
                                                     op0=ALU.mult, op1=ALU.mult),
              reads=tA.keys() + tB.keys(), writes=tA.keys())
        S.add("dve", lambda e: e.tensor_tensor(out=xb.t[:, :, :], in0=xb.t[:, :, :],
                                               in1=tB.t[:, None, :].broadcast_to([128, 8, TT]), op=ALU.mult),
              reads=xk + tB.keys(), writes=xk)
        S.add("pool", lambda e: e.tensor_tensor(out=xb.t[:, :, :], in0=xb.t[:, :, :],
                                                in1=tA.t[:, None, :].broadcast_to([128, 8, TT]), op=ALU.add),
              reads=xk + tA.keys(), writes=xk)
        TMP.free(tA, tB)

        def affine(e):
            ins = None
            for kc in range(8):
                ins = e.activation(out=xb.t[:, kc, :], in_=xb.t[:, kc, :], func=AF.Identity, scale=cc(gcol + kc),
                                   bias=cc(bcol + kc))
            return ins
        S.add("act", affine, reads=xk + KCST, writes=xk)
        if final:
            S.add("sp", lambda e: e.dma_start(
                out=outT[seq].rearrange("(kc p) t -> p kc t", p=128)[:, :, tt * TT:(tt + 1) * TT], in_=xb.t[:, :, :]),
                reads=xk, writes=[("out", seq, tt)], dma=("out", tt % 2))
        else:
            S.add("act", lambda e: e.activation(out=HI.t[:, :, tt * TT:(tt + 1) * TT], in_=xb.t[:, :, :], func=AF.Copy),
                  reads=xk, writes=hik(tt))
            S.add("dve", lambda e: e.tensor_tensor(out=LO.t[:, :, tt * TT:(tt + 1) * TT], in0=xb.t[:, :, :],
                                                   in1=HI.t[:, :, tt * TT:(tt + 1) * TT], op=ALU.subtract),
                  reads=xk + hik(tt), writes=lok(tt))

    RES_EPS = LN_EPS / (ALPHA * ALPHA)
    INV_A = 1.0 / ALPHA

    def res_pairs(c, tt):
        return [(ident_bf, HI.t[:, c, tt * TT:(tt + 1) * TT]), (ident_bf, LO.t[:, c, tt * TT:(tt + 1) * TT])]

    def res_keys(c, tt):
        return HI.keys3([c], tt * TT, (tt + 1) * TT) + LO.keys3([c], tt * TT, (tt + 1) * TT) + KM16

    S.add("sp", lambda e: e.dma_start(out=CST.t[:, :], in_=cstd[:, :]), writes=KCST, dma=("c", 0))
    S.add("sp", lambda e: e.dma_start(out=M32.t[:, :], in_=m32d[:, :]), writes=KM32, dma=("c", 1))
    S.add("pool", lambda e: e.dma_start(out=M16.t[:, :], in_=m16d[:, :]), writes=KM16, dma=("c", 2))
    wq = WQ()

    for l in range(n_layers):
        tE = TMP.alloc()
        tC = TMP.alloc()
        tD = TMP.alloc()
        ev, av, tv = tE.t[:, 0:8], tC.t[:, 0:8], tD.t[:, 0:8]
        lam = CST.t[:, l * CPL + C_LAM: l * CPL + C_LAM + 8]
        S.add("act", lambda e, ev=ev, lam=lam: e.activation(out=ev, in_=lam, func=AF.Exp, scale=-1.0),
              reads=KCST, writes=tE.keys())
        coefs = [1.0 / 7, -1.0 / 6, 1.0 / 5, -1.0 / 4, 1.0 / 3, -1.0 / 2, 1.0]
        S.add("dve", lambda e, ev=ev, av=av: e.tensor_scalar(out=av, in0=ev, scalar1=-1.0 / 8, scalar2=coefs[0],
                                                            op0=ALU.mult, op1=ALU.add),
              reads=tE.keys(), writes=tC.keys())
        for cf in coefs[1:]:
            S.add("dve", lambda e, ev=ev, av=av, tv=tv: e.tensor_tensor(out=tv, in0=av, in1=ev, op=ALU.mult),
                  reads=tC.keys() + tE.keys(), writes=tD.keys())
            S.add("dve", lambda e, av=av, tv=tv, cf=cf: e.tensor_scalar(out=av, in0=tv, scalar1=cf, scalar2=None, op0=ALU.add),
                  reads=tD.keys(), writes=tC.keys())
        S.add("dve", lambda e, ev=ev, av=av, tv=tv: e.tensor_tensor(out=tv, in0=av, in1=ev, op=ALU.mult),
              reads=tC.keys() + tE.keys(), writes=tD.keys())
        S.add("dve", lambda e, tv=tv, l=l: e.tensor_scalar(out=DER.t[:, l * 16: l * 16 + 8], in0=tv, scalar1=-8.0, scalar2=None,
                                                          op0=ALU.mult),
              reads=tD.keys(), writes=DER.keys())
        S.add("dve", lambda e, tv=tv, l=l: e.tensor_scalar(out=DER.t[:, l * 16 + 8: l * 16 + 16], in0=tv, scalar1=-16.0,
                                                          scalar2=None, op0=ALU.mult),
              reads=tD.keys(), writes=DER.keys())
        TMP.free(tE, tC, tD)
    KDER = DER.keys()

    def phase_entry(seq):
        for tt in range(NTT):
            xb = X1W[tt % 2]
            S.add("sp", lambda e, xb=xb, tt=tt: e.dma_start(
                out=xb.t[:, :, :], in_=xT[seq].rearrange("(kc p) t -> p kc t", p=128)[:, :, tt * TT:(tt + 1) * TT]),
                writes=xb.keys(), dma=("x", tt % 2))
            layernorm(xb, C_ING, C_INB, tt)

    def phase_rnn(l, seq):
        cb = l * CPL
        S.add("pool", lambda e: e.memset(RXP.t[:, 0:3], 0.0), writes=RXP.keys(0, 3))
        sg = wq.get(l, SL_GATES)
        for c in range(8):
            s = wq.get(l, SL_RXY + c)
            wsk = WS[s].keys()
            for hf in range(2):
                u = 2 * c + hf
                t0 = hf * 1024
                xc, rr, xcb, ii, aa, hs = XC[u % 2], RR[u % 2], XCB[u % 2], II[u % 2], AA[u % 2], HS[u % 2]
                hs_prev = HS[(u + 1) % 2]
                for t2 in range(2):
                    tt = 2 * hf + t2
                    p = PS.alloc()
                    S.add("pe", mm_group(p.t[:, :], [(WS[s].t[:, kc * 256: kc * 256 + 128], HI.t[:, kc, tt * TT:(tt + 1) * TT])
                                                     for kc in range(8)]), reads=wsk + hik(tt), writes=[p.key])
                    S.add("act", lambda e, p=p, tt=tt: e.activation(out=RXP.t[:, 3 + tt * TT: 3 + (tt + 1) * TT], in_=p.t[:, :],
                                                                   func=AF.Copy),
                          reads=[p.key], writes=RXP.keys(3 + tt * TT, 3 + (tt + 1) * TT))
                    PS.free(p)
                S.add("pool", lambda e, c=c, xc=xc, t0=t0: e.tensor_scalar(
                    out=xc.t[:, :], in0=RXP.t[:, 3 + t0:3 + t0 + 1024], scalar1=cc(cb + C_CONVW + 24 + c),
                    scalar2=cc(cb + C_CONVB + c), op0=ALU.mult, op1=ALU.add),
                    reads=RXP.keys(3 + t0, 3 + t0 + 1024) + KCST, writes=xc.keys())
                for k in range(3):
                    S.add("dve", lambda e, c=c, k=k, xc=xc, t0=t0: e.scalar_tensor_tensor(
                        out=xc.t[:, :], in0=RXP.t[:, k + t0:k + t0 + 1024], scalar=cc(cb + C_CONVW + 8 * k + c), in1=xc.t[:, :],
                        op0=ALU.mult, op1=ALU.add),
                        reads=RXP.keys(k + t0, k + t0 + 1024) + xc.keys() + KCST, writes=xc.keys())
                S.add("dve", lambda e, xc=xc, xcb=xcb: e.tensor_copy(out=xcb.t[:, :], in_=xc.t[:, :]),
                      reads=xc.keys(), writes=xcb.keys())
                for which, dst, bcol in ((0, rr, C_BA), (1, ii, C_BX)):
                    for t2 in range(2):
                        p = PS.alloc()
                        S.add("pe", mm_group(p.t[:, :], [(WS[sg].t[:, c * 256 + which * 128: c * 256 + which * 128 + 128],
                                                          xcb.t[:, t2 * TT:(t2 + 1) * TT])]),
                              reads=WS[sg].keys() + xcb.keys(t2 * TT, (t2 + 1) * TT), writes=[p.key])
                        S.add("act", lambda e, p=p, t2=t2, dst=dst, bcol=bcol, c=c: e.activation(
                            out=dst.t[:, t2 * TT:(t2 + 1) * TT], in_=p.t[:, :], func=AF.Sigmoid, bias=cc(cb + bcol + c)),
                            reads=[p.key] + KCST, writes=dst.keys(t2 * TT, (t2 + 1) * TT))
                        PS.free(p)
                S.add("act", lambda e, c=c, aa=aa, rr=rr: e.activation(out=aa.t[:, :], in_=rr.t[:, :], func=AF.Exp,
                                                                      scale=DER.t[:, l * 16 + c: l * 16 + c + 1]),
                      reads=rr.keys() + KDER, writes=aa.keys())
                S.add("act", lambda e, c=c, rr=rr: e.activation(out=rr.t[:, :], in_=rr.t[:, :], func=AF.Exp,
                                                               scale=DER.t[:, l * 16 + 8 + c: l * 16 + 9 + c]),
                      reads=rr.keys() + KDER, writes=rr.keys())
                S.add("act", lambda e, rr=rr: e.activation(out=rr.t[:, :], in_=rr.t[:, :], func=AF.Ln, scale=-1.0, bias=1.0),
                      reads=rr.keys(), writes=rr.keys())
                S.add("act", lambda e, rr=rr: e.activation(out=rr.t[:, :], in_=rr.t[:, :], func=AF.Exp, scale=0.5),
                      reads=rr.keys(), writes=rr.keys())
                S.add("pool", lambda e, ii=ii, rr=rr: e.tensor_tensor(out=ii.t[:, :], in0=ii.t[:, :], in1=rr.t[:, :], op=ALU.mult),
                      reads=ii.keys() + rr.keys(), writes=ii.keys())
                S.add("pool", lambda e, ii=ii, xc=xc: e.tensor_tensor(out=ii.t[:, :], in0=ii.t[:, :], in1=xc.t[:, :], op=ALU.mult),
                      reads=ii.keys() + xc.keys(), writes=ii.keys())
                if hf == 0:
                    S.add("dve", lambda e, hs=hs, aa=aa, ii=ii: e.tensor_tensor_scan(
                        out=hs.t[:, :], data0=aa.t[:, :], data1=ii.t[:, :], initial=0.0, op0=ALU.mult, op1=ALU.add),
                        reads=aa.keys() + ii.keys(), writes=hs.keys())
                else:
                    S.add("dve", lambda e, hs=hs, aa=aa, ii=ii, hp=hs_prev: e.tensor_tensor_scan(
                        out=hs.t[:, :], data0=aa.t[:, :], data1=ii.t[:, :], initial=hp.t[:, 1023:1024], op0=ALU.mult, op1=ALU.add),
                        reads=aa.keys() + ii.keys() + hs_prev.keys(1023, 1024), writes=hs.keys())
                for t2 in range(2):
                    tt = 2 * hf + t2
                    p = PS.alloc()
                    S.add("pe", mm_group(p.t[:, :], [(WS[s].t[:, kc * 256 + 128: kc * 256 + 256], HI.t[:, kc, tt * TT:(tt + 1) * TT])
                                                     for kc in range(8)]), reads=wsk + hik(tt), writes=[p.key])
                    tG = TMP.alloc()
                    S.add("act", lambda e, p=p, tG=tG: e.activation(out=tG.t[:, :], in_=p.t[:, :], func=AF.Gelu_apprx_tanh),
                          reads=[p.key], writes=tG.keys())
                    S.add("dve", lambda e, tG=tG, hs=hs, c=c, tt=tt, t2=t2: e.tensor_tensor(
                        out=RNN.t[:, c, tt * TT:(tt + 1) * TT], in0=hs.t[:, t2 * TT:(t2 + 1) * TT], in1=tG.t[:, :], op=ALU.mult),
                        reads=hs.keys(t2 * TT, (t2 + 1) * TT) + tG.keys(), writes=RNN.keys3([c], tt * TT, (tt + 1) * TT))
                    PS.free(p)
                    TMP.free(tG)
            wq.done(s)
        wq.done(sg)

    def phase_attn(l, seq):
        sf = wq.get(l, SL_WF)
        pf = PS.alloc()

        def fproj(e):
            ins = None
            for tb in range(16):
                for kc in range(8):
                    ins = e.matmul(pf.t[:, tb * 8:(tb + 1) * 8], lhsT=HI.t[:, kc, tb * 128:(tb + 1) * 128],
                                   rhs=WS[sf].t[:, kc * 8:(kc + 1) * 8], start=(kc == 0), stop=(kc == 7))
            return ins
        S.add("pe", fproj, reads=HI.keys() + WS[sf].keys(), writes=[pf.key])
        wq.done(sf)
        S.add("dve", lambda e: e.tensor_tensor(
            out=XF.t[:, :, :], in0=pf.t[:, 0:128].rearrange("p (a b) -> p a b", b=8),
            in1=CST.t[:, None, C_BF + 8 * l: C_BF + 8 * l + 8].broadcast_to([128, 16, 8]), op=ALU.add),
            reads=[pf.key] + KCST, writes=XF.keys())
        PS.free(pf)
        S.add("act", lambda e: e.activation(out=XF.t[:, :, :], in_=XF.t[:, :, :], func=AF.Exp, scale=-1.0),
              reads=XF.keys(), writes=XF.keys())
        S.add("act", lambda e: e.activation(out=SPB.t[:, :, :], in_=XF.t[:, :, :], func=AF.Ln, bias=1.0),
              reads=XF.keys(), writes=SPB.keys())
        spflat = SPB.t[:, :, :].rearrange("p a b -> p (a b)")
        S.add("dve", lambda e: e.tensor_copy(out=SP3[0].t[:, :], in_=spflat), reads=SPB.keys(), writes=SP3[0].keys())
        S.add("dve", lambda e: e.tensor_tensor(out=R1B.t[:, :], in0=spflat, in1=SP3[0].t[:, :], op=ALU.subtract),
              reads=SPB.keys() + SP3[0].keys(), writes=R1B.keys())
        S.add("dve", lambda e: e.tensor_copy(out=SP3[1].t[:, :], in_=R1B.t[:, :]), reads=R1B.keys(), writes=SP3[1].keys())
        S.add("dve", lambda e: e.tensor_tensor(out=R1B.t[:, :], in0=R1B.t[:, :], in1=SP3[1].t[:, :], op=ALU.subtract),
              reads=R1B.keys() + SP3[1].keys(), writes=R1B.keys())
        S.add("dve", lambda e: e.tensor_copy(out=SP3[2].t[:, :], in_=R1B.t[:, :]), reads=R1B.keys(), writes=SP3[2].keys())
        sp3k = SP3[0].keys() + SP3[1].keys() + SP3[2].keys()
        pc = PS.alloc()
        pS = PS.alloc()
        S.add("pe", mm_group(pc.t[:, 0:128], [(tri_bf, SP3[i].t[:, :]) for i in range(3)]), reads=sp3k + KM16, writes=[pc.key])
        S.add("pe", mm_group(pS.t[:, 0:128], [(ones_bf, SP3[i].t[:, :]) for i in range(3)]), reads=sp3k + KM16, writes=[pS.key])
        S.add("dve", lambda e: e.tensor_copy(out=SBK.t[:, :, :], in_=pS.t[:, 0:128].rearrange("p (a b) -> p a b", b=8)),
              reads=[pS.key], writes=SBK.keys())

        def scans(e):
            ins = None
            for hd in range(8):
                ins = e.tensor_tensor_scan(out=INC.t[:, :, hd], data0=ones32, data1=SBK.t[:, :, hd], initial=0.0,
                                           op0=ALU.mult, op1=ALU.add)
            return ins
        S.add("dve", scans, reads=SBK.keys() + KM32, writes=INC.keys())
        S.add("dve", lambda e: e.tensor_tensor(out=EXB.t[:, :, :], in0=INC.t[:, :, :], in1=SBK.t[:, :, :], op=ALU.subtract),
              reads=INC.keys() + SBK.keys(), writes=EXB.keys())
        S.add("dve", lambda e: e.tensor_tensor(out=CSB.t[:, :, :], in0=pc.t[:, 0:128].rearrange("p (a b) -> p a b", b=8),
                                               in1=EXB.t[:, :, :], op=ALU.add),
              reads=[pc.key] + EXB.keys(), writes=CSB.keys())
        pm = PS.alloc()
        S.add("pe", mm_group(pm.t[:, 0:128], [(half_bf, SP3[i].t[:, :]) for i in range(3)]), reads=sp3k + KM16, writes=[pm.key])
        S.add("dve", lambda e: e.tensor_tensor(out=XF.t[:, :, :], in0=pm.t[:, 0:128].rearrange("p (a b) -> p a b", b=8),
                                               in1=EXB.t[:, :, :], op=ALU.add),
              reads=[pm.key] + EXB.keys(), writes=XF.keys())
        PS.free(pc, pS, pm)

        for hd in range(8):
            g = hd // 2
            vb = VB[g % 2]
            if hd % 2 == 0:
                sv = wq.get(l, SL_V + g)
                for t2 in range(8):
                    p = PS.alloc()

                    def vproj(e, p=p, t2=t2, sv=sv):
                        ins = None
                        for hh in range(2):
                            tb = 2 * t2 + hh
                            for kc in range(8):
                                ins = e.matmul(p.t[:, hh * 256:(hh + 1) * 256], lhsT=HI.t[:, kc, tb * 128:(tb + 1) * 128],
                                               rhs=WS[sv].t[:, kc * 256:(kc + 1) * 256], start=(kc == 0), stop=(kc == 7))
                        return ins
                    S.add("pe", vproj, reads=WS[sv].keys() + hik(t2 // 2), writes=[p.key])
                    S.add("dve", lambda e, p=p, t2=t2, vb=vb: e.tensor_copy(
                        out=vb.t[:, 2 * t2:2 * t2 + 2, :].rearrange("p a b -> p (a b)"), in_=p.t[:, :]),
                        reads=[p.key], writes=vb.keys(2 * t2 * 256, (2 * t2 + 2) * 256))
                    PS.free(p)
                wq.done(sv)
            sq = wq.get(l, SL_QK + hd)
            qt, kt, bt = QT[hd % 2], KT[hd % 2], BT[hd % 2]
            for tt in range(NTT):
                p = PS.alloc()
                S.add("pe", mm_group(p.t[:, :], [(WS[sq].t[:, kc * 256: kc * 256 + 128], HI.t[:, kc, tt * TT:(tt + 1) * TT])
                                                 for kc in range(8)]), reads=WS[sq].keys() + hik(tt), writes=[p.key])
                S.add("act", lambda e, p=p, tt=tt, qt=qt: e.activation(out=qt.t[:, tt * TT:(tt + 1) * TT], in_=p.t[:, :],
                                                                      func=AF.Copy, scale=1.0 / math.sqrt(128.0)),
                      reads=[p.key], writes=qt.keys(tt * TT, (tt + 1) * TT))
                PS.free(p)
                p = PS.alloc()
                S.add("pe", mm_group(p.t[:, :], [(WS[sq].t[:, kc * 256 + 128: kc * 256 + 256], HI.t[:, kc, tt * TT:(tt + 1) * TT])
                                                 for kc in range(8)]), reads=WS[sq].keys() + hik(tt), writes=[p.key])
                S.add("dve", lambda e, p=p, tt=tt, kt=kt: e.tensor_copy(out=kt.t[:, tt * TT:(tt + 1) * TT], in_=p.t[:, :]),
                      reads=[p.key], writes=kt.keys(tt * TT, (tt + 1) * TT))
                PS.free(p)
            wq.done(sq)

            def mkbt(e, hd=hd, bt=bt):
                ins = None
                for j in range(16):
                    ins = e.tensor_scalar(out=bt.t[:, 0:j + 1, j], in0=CSB.t[:, 0:j + 1, hd], scalar1=XF.t[:, j, hd:hd + 1],
                                          scalar2=None, op0=ALU.subtract)
                return ins
            S.add("pool", mkbt, reads=CSB.keys() + XF.keys(), writes=bt.keys())

            steps = [(Q, i) for Q in range(4) for i in range(4 * Q + 4)]
            state = {}
            pslot = [0]

            def emit_S(n):
                Q, i = steps[n]
                r = max(0, i - 4 * Q)
                ps = PS.alloc()
                sl = PSL[pslot[0] % 4]
                pslot[0] += 1

                def fS(e, ps=ps, Q=Q, i=i, r=r, kt=kt, qt=qt):
                    diag = i >= 4 * Q
                    ins = e.matmul(ps.t[:, r * 128:512], lhsT=kt.t[:, i * 128:(i + 1) * 128],
                                   rhs=qt.t[:, Q * TT + r * 128:(Q + 1) * TT], start=True, stop=not diag)
                    if diag:
                        ins = e.matmul(ps.t[:, r * 128:(r + 1) * 128], lhsT=ident_bf, rhs=maskneg_bf, start=False, stop=True)
                    return ins
                S.add("pe", fS, reads=kt.keys(i * 128, (i + 1) * 128) + qt.keys(Q * TT + r * 128, (Q + 1) * TT) + KM16,
                      writes=[ps.key])

                def fE(e, ps=ps, sl=sl, Q=Q, i=i, r=r, bt=bt):
                    ins = None
                    for jb in range(r, 4):
                        ins = e.activation(out=sl.t[:, jb * 128:(jb + 1) * 128], in_=ps.t[:, jb * 128:(jb + 1) * 128],
                                           func=AF.Exp, bias=bt.t[:, i, 4 * Q + jb:4 * Q + jb + 1])
                    return ins
                S.add("act", fE, reads=[ps.key] + bt.keys(), writes=sl.keys())
                PS.free(ps)
                state[n] = sl

            def emit_PV(n):
                Q, i = steps[n]
                r = max(0, i - 4 * Q)
                sl = state.pop(n)
                if i == 0:
                    state["O"] = PS.alloc()
                    state["R"] = PS.alloc()
                Ob, Rb = state["O"], state["R"]
                last = (i == 4 * Q + 3)

                def fPV(e, sl=sl, Ob=Ob, Rb=Rb, i=i, r=r, last=last, vb=vb, hd=hd):
                    e.matmul(Ob.t[:, r * 128:512], lhsT=vb.t[:, i, (hd % 2) * 128:(hd % 2) * 128 + 128],
                             rhs=sl.t[:, r * 128:512], start=(i == 0), stop=last)
                    return e.matmul(Rb.t[:, r * 128:512], lhsT=ones_bf, rhs=sl.t[:, r * 128:512], start=(i == 0), stop=last)
                S.add("pe", fPV, reads=sl.keys() + vb.keys(i * 256, (i + 1) * 256) + KM16, writes=[Ob.key, Rb.key])
                if last:
                    tR = TMP.alloc()
                    S.add("act", lambda e, tR=tR, Rb=Rb: e.activation(out=tR.t[:, :], in_=Rb.t[:, :], func=AF.Ln),
                          reads=[Rb.key], writes=tR.keys())
                    S.add("act", lambda e, tR=tR: e.activation(out=tR.t[:, :], in_=tR.t[:, :], func=AF.Exp, scale=-1.0),
                          reads=tR.keys(), writes=tR.keys())
                    S.add("dve", lambda e, tR=tR, Ob=Ob, Q=Q, hd=hd: e.tensor_tensor(out=ATT.t[:, hd, Q * TT:(Q + 1) * TT], in0=Ob.t[:, :],
                                                                             in1=tR.t[:, :], op=ALU.mult),
                          reads=[Ob.key] + tR.keys(), writes=ATT.keys3([hd], Q * TT, (Q + 1) * TT))
                    TMP.free(tR)
                    PS.free(Ob, Rb)

            SK = 2
            for n in range(len(steps) + SK):
                if n < len(steps):
                    emit_S(n)
                if n - SK >= 0:
                    emit_PV(n - SK)

    def phase_merge(l, seq):
        cb = l * CPL
        for s in range(4):
            sl4 = [wq.get(l, b + s) for b in (SL_GA, SL_GB, SL_BRA, SL_BRB)]
            srcs = [HI, HI, ATT, RNN]
            for tt in range(NTT):
                for c2 in range(2):
                    c = 2 * s + c2
                    pp = []
                    for w in range(4):
                        p = PS.alloc()
                        src = srcs[w]
                        S.add("pe", mm_group(p.t[:, :], [(WS[sl4[w]].t[:, kc * 256 + c2 * 128: kc * 256 + c2 * 128 + 128],
                                                          src.t[:, kc, tt * TT:(tt + 1) * TT]) for kc in range(8)]),
                              reads=WS[sl4[w]].keys() + src.keys3(range(8), tt * TT, (tt + 1) * TT), writes=[p.key])
                        pp.append(p)
                    tA = TMP.alloc()
                    tB = TMP.alloc()
                    S.add("act", lambda e, p=pp[0], tA=tA, c=c: e.activation(out=tA.t[:, :], in_=p.t[:, :], func=AF.Sigmoid,
                                                                            bias=cc(cb + C_BM0 + c)),
                          reads=[pp[0].key] + KCST, writes=tA.keys())
                    S.add("act", lambda e, p=pp[1], tB=tB, c=c: e.activation(out=tB.t[:, :], in_=p.t[:, :], func=AF.Sigmoid,
                                                                            bias=cc(cb + C_BM1 + c)),
                          reads=[pp[1].key] + KCST, writes=tB.keys())
                    S.add("dve", lambda e, p=pp[2], tA=tA: e.scalar_tensor_tensor(out=tA.t[:, :], in0=tA.t[:, :], scalar=INV_A,
                                                                                 in1=p.t[:, :], op0=ALU.mult, op1=ALU.mult),
                          reads=[pp[2].key] + tA.keys(), writes=tA.keys())
                    S.add("dve", lambda e, p=pp[3], tB=tB: e.scalar_tensor_tensor(out=tB.t[:, :], in0=tB.t[:, :], scalar=INV_A,
                                                                                 in1=p.t[:, :], op0=ALU.mult, op1=ALU.mult),
                          reads=[pp[3].key] + tB.keys(), writes=tB.keys())
                    S.add("pool", lambda e, tA=tA, tB=tB, c=c, tt=tt: e.tensor_tensor(
                        out=MG.t[:, c, tt * TT:(tt + 1) * TT], in0=tA.t[:, :], in1=tB.t[:, :], op=ALU.add),
                        reads=tA.keys() + tB.keys(), writes=MG.keys3([c], tt * TT, (tt + 1) * TT))
                    PS.free(*pp)
                    TMP.free(tA, tB)
            wq.done(*sl4)
        so = [wq.get(l, SL_WOUT + s) for s in range(4)]
        for tt in range(NTT):
            xb = X1A[tt % 2]
            for c in range(8):
                p = PS.alloc()
                ws = WS[so[c // 2]]
                S.add("pe", mm_group(p.t[:, :], [(ws.t[:, kc * 256 + (c % 2) * 128: kc * 256 + (c % 2) * 128 + 128],
                                                  MG.t[:, kc, tt * TT:(tt + 1) * TT]) for kc in range(8)] + res_pairs(c, tt)),
                      reads=ws.keys() + MG.keys3(range(8), tt * TT, (tt + 1) * TT) + res_keys(c, tt), writes=[p.key])
                S.add("act", lambda e, p=p, xb=xb, c=c: e.activation(out=xb.t[:, c, :], in_=p.t[:, :], func=AF.Copy),
                      reads=[p.key], writes=xb.keys3([c], 0, TT))
                PS.free(p)
            layernorm(xb, cb + C_MIXG, cb + C_MIXB, tt, eps=RES_EPS)
        wq.done(*so)

    def phase_ffn(l, seq):
        cb = l * CPL
        for half in range(2):
            for s in range(11):
                shg = wq.get(l, SL_HG + s)
                shu = wq.get(l, SL_HU + s)
                for t2 in range(2):
                    tt = half * 2 + t2
                    for c2 in range(2):
                        j = 2 * s + c2
                        pg = PS.alloc()
                        pu = PS.alloc()
                        for p, sl in ((pg, shg), (pu, shu)):
                            S.add("pe", mm_group(p.t[:, :], [(WS[sl].t[:, kc * 256 + c2 * 128: kc * 256 + c2 * 128 + 128],
                                                              HI.t[:, kc, tt * TT:(tt + 1) * TT]) for kc in range(8)]),
                                  reads=WS[sl].keys() + hik(tt), writes=[p.key])
                        tA = TMP.alloc()
                        S.add("act", lambda e, pg=pg, tA=tA: e.activation(out=tA.t[:, :], in_=pg.t[:, :], func=AF.Silu),
                              reads=[pg.key], writes=tA.keys())
                        S.add("dve", lambda e, pu=pu, tA=tA, j=j, t2=t2: e.scalar_tensor_tensor(
                            out=ACTT.t[:, j, t2 * TT:(t2 + 1) * TT], in0=tA.t[:, :], scalar=INV_A, in1=pu.t[:, :],
                            op0=ALU.mult, op1=ALU.mult),
                            reads=[pu.key] + tA.keys(), writes=ACTT.keys3([j], t2 * TT, (t2 + 1) * TT))
                        PS.free(pg, pu)
                        TMP.free(tA)
                wq.done(shg, shu)
            for c in range(8):
                s0 = wq.get(l, SL_FFO + 2 * c)
                s1 = wq.get(l, SL_FFO + 2 * c + 1)
                for t2 in range(2):
                    tt = half * 2 + t2
                    p = PS.alloc()
                    pairs = []
                    for j in range(NFF):
                        ws = WS[s0] if j < 11 else WS[s1]
                        pairs.append((ws.t[:, (j % 11) * 128:(j % 11) * 128 + 128], ACTT.t[:, j, t2 * TT:(t2 + 1) * TT]))
                    S.add("pe", mm_group(p.t[:, :], pairs + res_pairs(c, tt)),
                          reads=WS[s0].keys() + WS[s1].keys() + ACTT.keys3(range(NFF), t2 * TT, (t2 + 1) * TT) + res_keys(c, tt),
                          writes=[p.key])
                    S.add("act", lambda e, p=p, xb=X1W[t2], c=c: e.activation(out=xb.t[:, c, :], in_=p.t[:, :], func=AF.Copy),
                          reads=[p.key], writes=X1W[t2].keys3([c], 0, TT))
                    PS.free(p)
                wq.done(s0, s1)
            for t2 in range(2):
                layernorm(X1W[t2], cb + C_FFNG, cb + C_FFNB, half * 2 + t2, eps=RES_EPS)

    def phase_ple(l, seq, last):
        cb = l * CPL
        S.add("pool", lambda e: e.dma_start(out=PTB.t[:, :, :], in_=pT[l, seq].rearrange("(kc p) t -> p kc t", p=128)),
              writes=PTB.keys(), dma=("p", 0))
        sp4 = [wq.get(l, SL_WPG + s) for s in range(4)]
        sw = wq.get(l, SL_WPLE)
        for tt in range(NTT):
            xb = X1W[tt % 2]
            for c in range(8):
                pg = PS.alloc()
                pe_ = PS.alloc()
                ws = WS[sp4[c // 2]]
                S.add("pe", mm_group(pg.t[:, :], [(ws.t[:, kc * 256 + (c % 2) * 128: kc * 256 + (c % 2) * 128 + 128],
                                                   HI.t[:, kc, tt * TT:(tt + 1) * TT]) for kc in range(8)]),
                      reads=ws.keys() + hik(tt), writes=[pg.key])
                S.add("pe", mm_group(pe_.t[:, :], [(WS[sw].t[:, kc * 1024 + c * 128: kc * 1024 + c * 128 + 128],
                                                    PTB.t[:, kc, tt * TT:(tt + 1) * TT]) for kc in range(2)]),
                      reads=WS[sw].keys() + PTB.keys3(range(2), tt * TT, (tt + 1) * TT), writes=[pe_.key])
                tA = TMP.alloc()
                S.add("act", lambda e, pg=pg, tA=tA, c=c: e.activation(out=tA.t[:, :], in_=pg.t[:, :], func=AF.Sigmoid,
                                                                      bias=cc(cb + C_BPG + c)),
                      reads=[pg.key] + KCST, writes=tA.keys())
                S.add("dve", lambda e, pe_=pe_, tA=tA: e.scalar_tensor_tensor(out=tA.t[:, :], in0=tA.t[:, :], scalar=INV_A,
                                                                             in1=pe_.t[:, :], op0=ALU.mult, op1=ALU.mult),
                      reads=[pe_.key] + tA.keys(), writes=tA.keys())
                pr = PS.alloc()
                S.add("pe", mm_group(pr.t[:, :], res_pairs(c, tt)), reads=res_keys(c, tt), writes=[pr.key])
                S.add("dve", lambda e, pr=pr, tA=tA, xb=xb, c=c: e.tensor_tensor(out=xb.t[:, c, :], in0=tA.t[:, :], in1=pr.t[:, :],
                                                                                op=ALU.add),
                      reads=[pr.key] + tA.keys(), writes=xb.keys3([c], 0, TT))
                PS.free(pg, pe_, pr)
                TMP.free(tA)
            layernorm(xb, cb + C_PLEG, cb + C_PLEB, tt, final=last, seq=seq, eps=RES_EPS)
        wq.done(*sp4)
        wq.done(sw)

    for seq in range(n_seq):
        phase_entry(seq)
        for l in range(n_layers):
            phase_rnn(l, seq)
            phase_attn(l, seq)
            phase_merge(l, seq)
            phase_ffn(l, seq)
            phase_ple(l, seq, last=(l == n_layers - 1))
    assert wq.cons == len(wq.seq), (wq.cons, len(wq.seq))

    outk = [("out", s, tt) for s in range(n_seq) for tt in range(NTT)]

    def fin(e):
        return e.nop()
    S.add("sp", fin, reads=outk, writes=[("fin",)])
    S.emit(nc, reorder=REORDER)
    return nc


_CACHE = {}


def kernel(**inputs):
    inp = {k: np.asarray(v) for k, v in inputs.items()}
    x = inp["x"].astype(np.float32, copy=False)
    p = inp["p"].astype(np.float32, copy=False)
    B = x.shape[0]
    nseq = B // NCORES
    W = _build_slabs(inp)
    C = _build_consts(inp)
    m16, m32 = _build_masks()
    xT = np.ascontiguousarray(x.transpose(0, 2, 1))
    pTr = np.ascontiguousarray(p.transpose(0, 1, 3, 2))
    if "nc" not in _CACHE:
        _CACHE["nc"] = build_program(L, nseq)
    nc = _CACHE["nc"]
    in_maps = []
    for c in range(NCORES):
        in_maps.append({
            "xT": xT[c * nseq:(c + 1) * nseq],
            "pT": np.ascontiguousarray(pTr[:, c * nseq:(c + 1) * nseq]),
            "W": W, "cst": C, "m16": m16, "m32": m32,
        })
    res = run_bass_kernel_spmd(nc, in_maps, core_ids=list(range(NCORES)))
    outs = [res.results[c]["outT"] for c in range(NCORES)]
    oT = np.concatenate(outs, axis=0)
    return np.ascontiguousarray(oT.transpose(0, 2, 1)).astype(np.float32, copy=False)
```

```python
import math
import numpy as np
import concourse.bass as bass
import concourse.mybir as mybir
from concourse.bass_utils import run_bass_kernel_spmd

F32 = mybir.dt.float32
BF16 = mybir.dt.bfloat16
U8 = mybir.dt.uint8
AF = mybir.ActivationFunctionType
ALU = mybir.AluOpType

L = 4
D = 1024
T = 2048
KC = 8
TT = 512
NTT = 4
NH = 8
NCORES = 8
ALPHA = float((2 * L) ** 0.25)
LN_EPS = 1e-5
D_FF = 2816
NFF = 22
BLK = 512
WINDOWS = {"pe": 48, "act": 48, "dve": 48, "pool": 48, "sp": 48}
REORDER = True

CPL = 136
C_MIXG, C_MIXB, C_FFNG, C_FFNB, C_PLEG, C_PLEB = 0, 8, 16, 24, 32, 40
C_CONVW, C_CONVB, C_BA, C_BX, C_LAM, C_BM0, C_BM1, C_BPG = 48, 80, 88, 96, 104, 112, 120, 128
C_ING = L * CPL
C_INB = C_ING + 8
C_BF = C_INB + 8
NC = C_BF + L * 8

SL_RXY, SL_GATES, SL_V, SL_QK, SL_WF = 0, 8, 9, 13, 21
SL_GA, SL_GB, SL_BRA, SL_BRB, SL_WOUT = 22, 26, 30, 34, 38
SL_HG, SL_HU, SL_FFO, SL_WPG, SL_WPLE = 42, 53, 64, 80, 84
NSLAB = 85
SLAB_ELEMS = 2048
NSLOT = 6


def _slabk(m):
    k = m.shape[0] // 128
    n = m.shape[1]
    return np.ascontiguousarray(m.reshape(k, 128, n).transpose(1, 0, 2)).reshape(128, k * n)


def _build_slabs(inp):
    W = np.zeros((L, NSLAB, 128, SLAB_ELEMS), np.float32)
    for l in range(L):
        w_in = inp["w_in"][l]

        def put(si, arr):
            W[l, si, :, : arr.shape[1]] = arr

        for c in range(8):
            put(SL_RXY + c, _slabk(np.concatenate(
                [w_in[:, 3080 + c * 128: 3080 + (c + 1) * 128], w_in[:, 4104 + c * 128: 4104 + (c + 1) * 128]], axis=1)))
        ga = np.concatenate([inp["rg_w_a"][l], inp["rg_w_x"][l]], axis=2)
        put(SL_GATES, np.ascontiguousarray(ga.transpose(1, 0, 2)).reshape(128, 8 * 256))
        for g in range(4):
            put(SL_V + g, _slabk(w_in[:, 2048 + g * 256: 2048 + (g + 1) * 256]))
        for hd in range(8):
            put(SL_QK + hd, _slabk(np.concatenate(
                [w_in[:, hd * 128:(hd + 1) * 128], w_in[:, 1024 + hd * 128: 1024 + (hd + 1) * 128]], axis=1)))
        put(SL_WF, _slabk(w_in[:, 3072:3080]))
        for s in range(4):
            put(SL_GA + s, _slabk(w_in[:, 5128 + s * 256: 5128 + (s + 1) * 256]))
            put(SL_GB + s, _slabk(w_in[:, 6152 + s * 256: 6152 + (s + 1) * 256]))
            put(SL_BRA + s, _slabk(inp["w_branch_att"][l][:, s * 256:(s + 1) * 256]))
            put(SL_BRB + s, _slabk(inp["w_branch_rnn"][l][:, s * 256:(s + 1) * 256]))
            put(SL_WOUT + s, _slabk(inp["w_out"][l][:, s * 256:(s + 1) * 256]))
            put(SL_WPG + s, _slabk(inp["w_ple_gate"][l][:, s * 256:(s + 1) * 256]))
        wfi = inp["w_ffn_in"][l]
        for s in range(11):
            put(SL_HG + s, _slabk(wfi[:, s * 256:(s + 1) * 256]))
            put(SL_HU + s, _slabk(wfi[:, D_FF + s * 256: D_FF + (s + 1) * 256]))
        wfo = inp["w_ffn_out"][l]
        for c in range(8):
            for h in range(2):
                put(SL_FFO + 2 * c + h, _slabk(wfo[h * 1408:(h + 1) * 1408, c * 128:(c + 1) * 128]))
        put(SL_WPLE, _slabk(inp["w_ple"][l]))
    return W


def _fm(v):
    return np.ascontiguousarray(np.asarray(v, np.float32).reshape(8, 128).T)


def _build_consts(inp):
    C = np.zeros((128, NC), np.float32)
    for l in range(L):
        b = l * CPL
        C[:, b + C_MIXG: b + C_MIXG + 8] = _fm(inp["ln_mix_g"][l])
        C[:, b + C_MIXB: b + C_MIXB + 8] = _fm(inp["ln_mix_b"][l])
        C[:, b + C_FFNG: b + C_FFNG + 8] = _fm(inp["ln_ffn_g"][l])
        C[:, b + C_FFNB: b + C_FFNB + 8] = _fm(inp["ln_ffn_b"][l])
        C[:, b + C_PLEG: b + C_PLEG + 8] = _fm(inp["ln_ple_g"][l])
        C[:, b + C_PLEB: b + C_PLEB + 8] = _fm(inp["ln_ple_b"][l])
        for k in range(4):
            C[:, b + C_CONVW + 8 * k: b + C_CONVW + 8 * k + 8] = _fm(inp["conv_w"][l][k])
        C[:, b + C_CONVB: b + C_CONVB + 8] = _fm(inp["conv_b"][l])
        C[:, b + C_BA: b + C_BA + 8] = _fm(inp["rg_b_a"][l])
        C[:, b + C_BX: b + C_BX + 8] = _fm(inp["rg_b_x"][l])
        C[:, b + C_LAM: b + C_LAM + 8] = _fm(inp["rg_lambda"][l])
        C[:, b + C_BM0: b + C_BM0 + 8] = _fm(inp["b_merge"][l][0])
        C[:, b + C_BM1: b + C_BM1 + 8] = _fm(inp["b_merge"][l][1])
        C[:, b + C_BPG: b + C_BPG + 8] = _fm(inp["b_ple_gate"][l])
        C[:, C_BF + 8 * l: C_BF + 8 * l + 8] = np.broadcast_to(np.asarray(inp["b_forget"][l], np.float32)[None, :], (128, 8))
    C[:, C_ING: C_ING + 8] = _fm(inp["ln_in_g"])
    C[:, C_INB: C_INB + 8] = _fm(inp["ln_in_b"])
    return C


def _build_masks():
    r = np.arange(128)[:, None]
    c = np.arange(128)[None, :]
    ident = (r == c).astype(np.float32)
    maskneg = np.where(c < r, -30000.0, 0.0).astype(np.float32)
    ones = np.ones((128, 128), np.float32)
    tri = (r <= c).astype(np.float32)
    half = np.broadcast_to((r <= 63).astype(np.float32), (128, 128))
    return np.ascontiguousarray(np.concatenate([ident, maskneg, ones, tri, half], axis=1)), np.ones((128, 16), np.float32)


class _Op:
    __slots__ = ("idx", "eng", "fn", "deps", "dma", "needed", "token")

    def __init__(self, idx, eng, fn, deps, dma):
        self.idx, self.eng, self.fn, self.deps, self.dma = idx, eng, fn, deps, dma
        self.needed = False
        self.token = None


_ACT_SET = {AF.Sigmoid: "sig", AF.Exp: "exp", AF.Ln: "exp", AF.Gelu_apprx_tanh: "gelu", AF.Silu: "silu"}
ACT_SWITCH_NS = 1300.0


class _Probe:
    def __init__(self, eng):
        self.eng = eng
        self.cost = 0.0
        self.xfer = 0.0
        self.aset = None

    def then_inc(self, *a, **k):
        return self

    def matmul(self, out, lhsT, rhs, **kw):
        n = int(np.prod(rhs.shape[1:]))
        mult = 4.0 if rhs.tensor.dtype == F32 else 1.0
        self.cost += (max(n, 64) * 0.5 + 12.0) * mult
        return self

    def dma_start(self, out, in_, **kw):
        nbytes = float(np.prod(in_.shape)) * 4.0
        self.cost += 900.0 if self.eng == "pool" else 150.0
        self.xfer += 2200.0 + nbytes / 220.0
        return self

    def nop(self, *a, **k):
        self.cost += 50.0
        return self

    def _ew(self, out, scan=False):
        n = float(np.prod(out.shape[1:]))
        if self.eng == "act":
            self.cost += 230.0 + 0.75 * n
        elif self.eng == "dve":
            self.cost += 110.0 + (2.1 if scan else 1.05) * n
        else:
            self.cost += 300.0 + 2.15 * n
        return self

    def activation(self, out, in_, func, **kw):
        st = _ACT_SET.get(func)
        if st is not None:
            self.aset = st
        return self._ew(out)

    def tensor_tensor_scan(self, out, **kw):
        return self._ew(out, scan=True)

    def tensor_tensor(self, out, **kw):
        return self._ew(out)

    def tensor_scalar(self, out, **kw):
        return self._ew(out)

    def scalar_tensor_tensor(self, out, **kw):
        return self._ew(out)

    def tensor_copy(self, out, **kw):
        return self._ew(out)

    def reciprocal(self, out, **kw):
        return self._ew(out)

    def memset(self, ap, *a, **kw):
        return self._ew(ap)


class Sched:
    ENGS = ("pe", "act", "dve", "pool", "sp")

    def __init__(self):
        self.ops = []
        self.last_w = {}
        self.readers = {}

    def add(self, eng, fn, reads=(), writes=(), dma=None):
        idx = len(self.ops)
        deps = {}
        for r in reads:
            w = self.last_w.get(r)
            if w is not None:
                deps[w] = True
        for k in writes:
            w = self.last_w.get(k)
            if w is not None and w not in deps:
                deps[w] = False
            for rd in self.readers.get(k, ()):
                if rd not in deps:
                    deps[rd] = False
        deps.pop(idx, None)
        for r in reads:
            self.readers.setdefault(r, []).append(idx)
        for k in writes:
            self.last_w[k] = idx
            self.readers[k] = []
        self.ops.append(_Op(idx, eng, fn, deps, dma))
        return idx

    def _list_schedule(self, lat=180.0):
        ops = self.ops
        n = len(ops)
        cost = [0.0] * n
        xfer = [0.0] * n
        aset = [None] * n
        cur_set = [None]
        for op in ops:
            pr = _Probe(op.eng)
            op.fn(pr)
            cost[op.idx] = pr.cost
            xfer[op.idx] = pr.xfer
            aset[op.idx] = pr.aset
        succ = [[] for _ in range(n)]
        nleft = [0] * n
        for op in ops:
            nleft[op.idx] = len(op.deps)
            for d in op.deps:
                succ[d].append(op.idx)
        by_eng = {e: [op.idx for op in ops if op.eng == e] for e in self.ENGS}
        ptr = {e: 0 for e in self.ENGS}
        done = [False] * n
        ready = [0.0] * n
        finish = [0.0] * n
        efree = {e: 0.0 for e in self.ENGS}
        order = {e: [] for e in self.ENGS}
        remaining = n
        while remaining:
            best = None
            for e in self.ENGS:
                lst = by_eng[e]
                i = ptr[e]
                while i < len(lst) and done[lst[i]]:
                    i += 1
                ptr[e] = i
                cnt = 0
                j = i
                ef = efree[e]
                while j < len(lst) and cnt < WINDOWS[e]:
                    k = lst[j]
                    j += 1
                    if done[k]:
                        continue
                    cnt += 1
                    if nleft[k]:
                        continue
                    st = ready[k] if ready[k] > ef else ef
                    if e == "act":
                        sw = aset[k] is not None and aset[k] != cur_set[0]
                        sc = st + (ACT_SWITCH_NS if sw else 0.0)
                        if best is None or sc < best[0]:
                            best = (sc, e, k, st, sw)
                        if sc <= ef:
                            break
                    else:
                        if best is None or st < best[0]:
                            best = (st, e, k, st, False)
                        if st <= ef:
                            break
            _sc, e, k, st, sw = best
            done[k] = True
            remaining -= 1
            if sw:
                st += ACT_SWITCH_NS
            if e == "act" and aset[k] is not None:
                cur_set[0] = aset[k]
            efree[e] = st + cost[k]
            finish[k] = st + cost[k] + xfer[k]
            order[e].append(k)
            for sidx in succ[k]:
                nleft[sidx] -= 1
                so = ops[sidx]
                same = (so.eng == e) and (ops[k].dma is None) and (so.dma is None)
                f = finish[k] + (0.0 if same else lat)
                if f > ready[sidx]:
                    ready[sidx] = f
        self.sim_time = max(finish) if n else 0.0
        return order

    def emit(self, nc, reorder=True):
        ops = self.ops
        if reorder:
            order = self._list_schedule()
        else:
            order = {e: [op.idx for op in ops if op.eng == e] for e in self.ENGS}
        for op in ops:
            keep = {}
            for d, raw in op.deps.items():
                p = ops[d]
                same = (p.eng == op.eng) and (p.dma is None) and (op.dma is None)
                if same and op.eng == "pe":
                    continue
                keep[d] = raw
            op.deps = keep
            for d in keep:
                ops[d].needed = True
        engs = list(self.ENGS)
        dma_keys = sorted({op.dma for op in ops if op.dma is not None}, key=str)
        sem_names = ["e_" + e for e in engs[:4]] + ["d_%d" % i for i in range(len(dma_keys))]
        from contextlib import ExitStack
        with ExitStack() as st:
            sems = [st.enter_context(nc.semaphore(n)) for n in sem_names]
            esem = {e: sems[i] for i, e in enumerate(engs[:4])}
            dsem = {k: sems[4 + i] for i, k in enumerate(dma_keys)}
            cnt = {}
            for e in engs:
                for k in order[e]:
                    op = ops[k]
                    if op.dma is not None:
                        cnt[op.dma] = cnt.get(op.dma, 0) + 16
                        op.token = (dsem[op.dma], cnt[op.dma])
                    elif op.needed:
                        cnt[op.eng] = cnt.get(op.eng, 0) + 1
                        op.token = (esem[op.eng], cnt[op.eng])
            block = st.enter_context(nc.Block())

            def run(eng_name):
                def body(e):
                    waited = {}
                    for k in order[eng_name]:
                        op = ops[k]
                        for d in op.deps:
                            s, v = ops[d].token
                            key = id(s)
                            if waited.get(key, 0) < v:
                                e.wait_ge(s, v)
                                waited[key] = v
                        ins = op.fn(e)
                        if op.dma is not None:
                            ins.then_inc(op.token[0], 16)
                        elif op.needed:
                            ins.then_inc(op.token[0], 1)
                return body

            block.tensor(run("pe"))
            block.scalar(run("act"))
            block.vector(run("dve"))
            block.gpsimd(run("pool"))
            block.sync(run("sp"))


class _Pool:
    def __init__(self, items):
        self.free_list = list(items)

    def alloc(self):
        return self.free_list.pop(0)

    def free(self, *its):
        for it in its:
            self.free_list.append(it)


class Buf:
    def __init__(self, nc, name, base, off, shape, dtype):
        self.t = nc.alloc_sbuf_tensor_at(name, shape, dtype, offset=base + off)
        self.off = off
        self.esz = 2 if dtype == BF16 else 4
        self.row = int(np.prod(shape[1:]))
        self.inner = shape[-1]
        self.nbytes = self.row * self.esz

    def keys(self, lo=0, hi=None):
        if hi is None:
            hi = self.row
        a = self.off + lo * self.esz
        b = self.off + hi * self.esz
        return [("sb", i) for i in range(a // BLK, (b + BLK - 1) // BLK)]

    def keys3(self, chunks, lo, hi):
        out = []
        for c in chunks:
            out += self.keys(c * self.inner + lo, c * self.inner + hi)
        return out


class PsBank:
    def __init__(self, nc, i):
        self.t = nc.alloc_psum_tensor("ps%d" % i, [128, 512], F32)
        self.key = ("ps", i)


def build_program(n_layers=L, n_seq=2, dbg=False):
    nc = bass.Bass("TRN2", target_bir_lowering=False)
    xT = nc.dram_tensor("xT", [n_seq, D, T], F32, kind="ExternalInput").ap()
    pT = nc.dram_tensor("pT", [L, n_seq, 256, T], F32, kind="ExternalInput").ap()
    Wd = nc.dram_tensor("W", [L, NSLAB, 128, SLAB_ELEMS], F32, kind="ExternalInput").ap()
    cstd = nc.dram_tensor("cst", [128, NC], F32, kind="ExternalInput").ap()
    m16d = nc.dram_tensor("m16", [128, 640], F32, kind="ExternalInput").ap()
    m32d = nc.dram_tensor("m32", [128, 16], F32, kind="ExternalInput").ap()
    outT = nc.dram_tensor("outT", [n_seq, D, T], F32, kind="ExternalOutput").ap()

    ARENA = 212480
    arena = nc.alloc_sbuf_tensor("arena", [128, ARENA], U8)
    base = nc.lookup_mloc(arena).addr

    def mk(name, off, shape, dtype):
        return Buf(nc, name, base, off, shape, dtype)

    O_HI, O_LO, O_A, O_B, O_W1, O_WS, O_PSL, O_TMP, O_MISC = 0, 32768, 65536, 98304, 131072, 163840, 188416, 192512, 200704
    HI = mk("hi", O_HI, [128, 8, T], BF16)
    LO = mk("lo", O_LO, [128, 8, T], BF16)
    ATT = mk("attT", O_A, [128, 8, T], BF16)
    RNN = mk("rnnT", O_B, [128, 8, T], BF16)
    RXP = mk("rxp", O_A, [128, T + 4], F32)
    XC = [mk("xc%d" % i, O_A + 8704 + 4096 * i, [128, 1024], F32) for i in range(2)]
    RR = [mk("rr%d" % i, O_A + 16896 + 4096 * i, [128, 1024], F32) for i in range(2)]
    XCB = [mk("xcb%d" % i, O_A + 25088 + 2048 * i, [128, 1024], BF16) for i in range(2)]
    II = [mk("ii%d" % i, O_W1 + 4096 * i, [128, 1024], F32) for i in range(2)]
    AA = [mk("aa%d" % i, O_W1 + 8192 + 4096 * i, [128, 1024], F32) for i in range(2)]
    HS = [mk("hs%d" % i, O_W1 + 16384 + 4096 * i, [128, 1024], F32) for i in range(2)]
    VB = [mk("vb%d" % i, O_W1 + 8192 * i, [128, 16, 256], BF16) for i in range(2)]
    QT = [mk("qt%d" % i, O_W1 + 16384 + 4096 * i, [128, T], BF16) for i in range(2)]
    KT = [mk("kt%d" % i, O_W1 + 24576 + 4096 * i, [128, T], BF16) for i in range(2)]
    MG = mk("mg", O_W1, [128, 8, T], BF16)
    X1W = [mk("x1w%d" % i, O_W1 + 16384 * i, [128, 8, TT], F32) for i in range(2)]
    X1A = [mk("x1a%d" % i, O_A + 16384 * i, [128, 8, TT], F32) for i in range(2)]
    ACTT = mk("actt", O_A, [128, NFF, 1024], BF16)
    PTB = mk("ptb", O_A, [128, 2, T], BF16)
    LNSQ = mk("lnsq", O_B + 16384, [128, 8, TT], BF16)
    LNXB = mk("lnxb", O_B + 24576, [128, 8, TT], BF16)
    WS = [mk("ws%d" % i, O_WS + 4096 * i, [128, SLAB_ELEMS], BF16) for i in range(NSLOT)]
    PSL = [mk("psl%d" % i, O_PSL + 1024 * i, [128, TT], BF16) for i in range(4)]
    TMPS = [mk("tmp%d" % i, O_TMP + 2048 * i, [128, TT], F32) for i in range(4)]
    CST = mk("cstb", O_MISC, [128, NC], F32)
    DER = mk("der", O_MISC + 2560, [128, L * 16], F32)
    M16 = mk("m16b", O_MISC + 3072, [128, 640], BF16)
    SP3 = [mk("sp3_%d" % i, O_MISC + 4352 + 256 * i, [128, 128], BF16) for i in range(3)]
    R1B = mk("r1b", O_MISC + 10240, [128, 128], F32)
    M32 = mk("m32b", O_MISC + 2816, [128, 16], F32)
    XF = mk("xf", O_MISC + 5120, [128, 16, 8], F32)
    SPB = mk("spb", O_MISC + 5632, [128, 16, 8], F32)
    SBK = mk("sbk", O_MISC + 6144, [128, 16, 8], F32)
    INC = mk("inc", O_MISC + 6656, [128, 16, 8], F32)
    EXB = mk("exb", O_MISC + 7168, [128, 16, 8], F32)
    CSB = mk("csb", O_MISC + 7680, [128, 16, 8], F32)
    BT = [mk("bt%d" % i, O_MISC + 8192 + 1024 * i, [128, 16, 16], F32) for i in range(2)]

    PSB = [PsBank(nc, i) for i in range(8)]
    PS = _Pool(PSB)
    TMP = _Pool(TMPS)
    S = Sched()

    ident_bf = M16.t[:, 0:128]
    maskneg_bf = M16.t[:, 128:256]
    ones_bf = M16.t[:, 256:384]
    tri_bf = M16.t[:, 384:512]
    half_bf = M16.t[:, 512:640]
    ones32 = M32.t[:, 0:16]
    KM16 = M16.keys()
    KM32 = M32.keys()
    KCST = CST.keys()

    def cc(col):
        return CST.t[:, col:col + 1]

    def slab_sequence():
        seq = []
        for _s in range(n_seq):
            for l in range(n_layers):
                seq.append((l, SL_GATES, 2048))
                for c in range(8):
                    seq.append((l, SL_RXY + c, 2048))
                seq.append((l, SL_WF, 64))
                for hd in range(8):
                    if hd % 2 == 0:
                        seq.append((l, SL_V + hd // 2, 2048))
                    seq.append((l, SL_QK + hd, 2048))
                for s in range(4):
                    for b in (SL_GA, SL_GB, SL_BRA, SL_BRB):
                        seq.append((l, b + s, 2048))
                for s in range(4):
                    seq.append((l, SL_WOUT + s, 2048))
                for _h in range(2):
                    for s in range(11):
                        seq.append((l, SL_HG + s, 2048))
                        seq.append((l, SL_HU + s, 2048))
                    for c in range(8):
                        seq.append((l, SL_FFO + 2 * c, 1408))
                        seq.append((l, SL_FFO + 2 * c + 1, 1408))
                for s in range(4):
                    seq.append((l, SL_WPG + s, 2048))
                seq.append((l, SL_WPLE, 2048))
        return seq

    class WQ:
        def __init__(self):
            self.seq = slab_sequence()
            self.nxt = 0
            self.cons = 0
            self.slot_of = {}
            for i in range(NSLOT):
                self._load(i)

        def _load(self, slot):
            if self.nxt >= len(self.seq):
                return
            l, si, n = self.seq[self.nxt]
            self.slot_of[self.nxt] = slot
            self.nxt += 1
            ws = WS[slot]
            S.add("pool", lambda e, ws=ws, l=l, si=si, n=n: e.dma_start(out=ws.t[:, 0:n], in_=Wd[l, si, :, 0:n]),
                  writes=ws.keys(), dma=("ws", slot))

        def get(self, l, si):
            assert self.seq[self.cons][:2] == (l, si), (self.seq[self.cons], l, si)
            slot = self.slot_of.pop(self.cons)
            self.cons += 1
            return slot

        def done(self, *slots):
            for s in slots:
                self._load(s)

    def mm_group(out_ap, pairs):
        def fn(e, out_ap=out_ap, pairs=pairs):
            n = len(pairs)
            ins = None
            for i, (lt, rh) in enumerate(pairs):
                ins = e.matmul(out_ap, lhsT=lt, rhs=rh, start=(i == 0), stop=(i == n - 1))
            return ins
        return fn

    def hik(tt):
        return HI.keys3(range(8), tt * TT, (tt + 1) * TT)

    def lok(tt):
        return LO.keys3(range(8), tt * TT, (tt + 1) * TT)

    def layernorm(xb, gcol, bcol, tt, final=False, seq=0, eps=LN_EPS):
        xk = xb.keys()
        S.add("act", lambda e: e.activation(out=LNSQ.t[:, :, :], in_=xb.t[:, :, :], func=AF.Square),
              reads=xk, writes=LNSQ.keys())
        S.add("act", lambda e: e.activation(out=LNXB.t[:, :, :], in_=xb.t[:, :, :], func=AF.Copy),
              reads=xk, writes=LNXB.keys())
        p1 = PS.alloc()
        p2 = PS.alloc()
        S.add("pe", mm_group(p1.t[:, :], [(ones_bf, LNXB.t[:, kc, :]) for kc in range(8)]),
              reads=LNXB.keys() + KM16, writes=[p1.key])
        S.add("pe", mm_group(p2.t[:, :], [(ones_bf, LNSQ.t[:, kc, :]) for kc in range(8)]),
              reads=LNSQ.keys() + KM16, writes=[p2.key])
        tA = TMP.alloc()
        tB = TMP.alloc()
        S.add("act", lambda e: e.activation(out=tA.t[:, :], in_=p1.t[:, :], func=AF.Copy, scale=1.0 / D),
              reads=[p1.key], writes=tA.keys())
        S.add("act", lambda e: e.activation(out=tB.t[:, :], in_=p1.t[:, :], func=AF.Square, scale=1.0 / D),
              reads=[p1.key], writes=tB.keys())
        S.add("dve", lambda e: e.scalar_tensor_tensor(out=tB.t[:, :], in0=p2.t[:, :], scalar=1.0 / D, in1=tB.t[:, :],
                                                     op0=ALU.mult, op1=ALU.subtract),
              reads=[p2.key] + tB.keys(), writes=tB.keys())
        PS.free(p1, p2)
        S.add("dve", lambda e: e.tensor_scalar(out=tB.t[:, :], in0=tB.t[:, :], scalar1=float(eps), scalar2=None, op0=ALU.add),
              reads=tB.keys(), writes=tB.keys())
        S.add("act", lambda e: e.activation(out=tB.t[:, :], in_=tB.t[:, :], func=AF.Ln), reads=tB.keys(), writes=tB.keys())
        S.add("act", lambda e: e.activation(out=tB.t[:, :], in_=tB.t[:, :], func=AF.Exp, scale=-0.5),
              reads=tB.keys(), writes=tB.keys())
        S.add("dve", lambda e: e.scalar_tensor_tensor(out=tA.t[:, :], in0=tA.t[:, :], scalar=-1.0, in1=tB.t[:, :],
                                                     op0=ALU.mult, op1=ALU.mult),
              reads=tA.keys() + tB.keys(), writes=tA.keys())
        S.add("dve", lambda e: e.tensor_tensor(out=xb.t[:, :, :], in0=xb.t[:, :, :],
                                               in1=tB.t[:, None, :].broadcast_to([128, 8, TT]), op=ALU.mult),
              reads=xk + tB.keys(), writes=xk)
        S.add("pool", lambda e: e.tensor_tensor(out=xb.t[:, :, :], in0=xb.t[:, :, :],
                                                in1=tA.t[:, None, :].broadcast_to([128, 8, TT]), op=ALU.add),
              reads=xk + tA.keys(), writes=xk)
        TMP.free(tA, tB)

        def affine(e):
            ins = None
            for kc in range(8):
                ins = e.activation(out=xb.t[:, kc, :], in_=xb.t[:, kc, :], func=AF.Identity, scale=cc(gcol + kc),
                                   bias=cc(bcol + kc))
            return ins
        S.add("act", affine, reads=xk + KCST, writes=xk)
        if final:
            S.add("sp", lambda e: e.dma_start(
                out=outT[seq].rearrange("(kc p) t -> p kc t", p=128)[:, :, tt * TT:(tt + 1) * TT], in_=xb.t[:, :, :]),
                reads=xk, writes=[("out", seq, tt)], dma=("out", tt % 2))
        else:
            S.add("act", lambda e: e.activation(out=HI.t[:, :, tt * TT:(tt + 1) * TT], in_=xb.t[:, :, :], func=AF.Copy),
                  reads=xk, writes=hik(tt))
            S.add("dve", lambda e: e.tensor_tensor(out=LO.t[:, :, tt * TT:(tt + 1) * TT], in0=xb.t[:, :, :],
                                                   in1=HI.t[:, :, tt * TT:(tt + 1) * TT], op=ALU.subtract),
                  reads=xk + hik(tt), writes=lok(tt))

    RES_EPS = LN_EPS / (ALPHA * ALPHA)
    INV_A = 1.0 / ALPHA

    def res_pairs(c, tt):
        return [(ident_bf, HI.t[:, c, tt * TT:(tt + 1) * TT]), (ident_bf, LO.t[:, c, tt * TT:(tt + 1) * TT])]

    def res_keys(c, tt):
        return HI.keys3([c], tt * TT, (tt + 1) * TT) + LO.keys3([c], tt * TT, (tt + 1) * TT) + KM16

    S.add("sp", lambda e: e.dma_start(out=CST.t[:, :], in_=cstd[:, :]), writes=KCST, dma=("c", 0))
    S.add("sp", lambda e: e.dma_start(out=M32.t[:, :], in_=m32d[:, :]), writes=KM32, dma=("c", 1))
    S.add("pool", lambda e: e.dma_start(out=M16.t[:, :], in_=m16d[:, :]), writes=KM16, dma=("c", 2))
    wq = WQ()

    for l in range(n_layers):
        tE = TMP.alloc()
        tC = TMP.alloc()
        tD = TMP.alloc()
        ev, av, tv = tE.t[:, 0:8], tC.t[:, 0:8], tD.t[:, 0:8]
        lam = CST.t[:, l * CPL + C_LAM: l * CPL + C_LAM + 8]
        S.add("act", lambda e, ev=ev, lam=lam: e.activation(out=ev, in_=lam, func=AF.Exp, scale=-1.0),
              reads=KCST, writes=tE.keys())
        coefs = [1.0 / 7, -1.0 / 6, 1.0 / 5, -1.0 / 4, 1.0 / 3, -1.0 / 2, 1.0]
        S.add("dve", lambda e, ev=ev, av=av: e.tensor_scalar(out=av, in0=ev, scalar1=-1.0 / 8, scalar2=coefs[0],
                                                            op0=ALU.mult, op1=ALU.add),
              reads=tE.keys(), writes=tC.keys())
        for cf in coefs[1:]:
            S.add("dve", lambda e, ev=ev, av=av, tv=tv: e.tensor_tensor(out=tv, in0=av, in1=ev, op=ALU.mult),
                  reads=tC.keys() + tE.keys(), writes=tD.keys())
            S.add("dve", lambda e, av=av, tv=tv, cf=cf: e.tensor_scalar(out=av, in0=tv, scalar1=cf, scalar2=None, op0=ALU.add),
                  reads=tD.keys(), writes=tC.keys())
        S.add("dve", lambda e, ev=ev, av=av, tv=tv: e.tensor_tensor(out=tv, in0=av, in1=ev, op=ALU.mult),
              reads=tC.keys() + tE.keys(), writes=tD.keys())
        S.add("dve", lambda e, tv=tv, l=l: e.tensor_scalar(out=DER.t[:, l * 16: l * 16 + 8], in0=tv, scalar1=-8.0, scalar2=None,
                                                          op0=ALU.mult),
              reads=tD.keys(), writes=DER.keys())
        S.add("dve", lambda e, tv=tv, l=l: e.tensor_scalar(out=DER.t[:, l * 16 + 8: l * 16 + 16], in0=tv, scalar1=-16.0,
                                                          scalar2=None, op0=ALU.mult),
              reads=tD.keys(), writes=DER.keys())
        TMP.free(tE, tC, tD)
    KDER = DER.keys()

    def phase_entry(seq):
        for tt in range(NTT):
            xb = X1W[tt % 2]
            S.add("sp", lambda e, xb=xb, tt=tt: e.dma_start(
                out=xb.t[:, :, :], in_=xT[seq].rearrange("(kc p) t -> p kc t", p=128)[:, :, tt * TT:(tt + 1) * TT]),
                writes=xb.keys(), dma=("x", tt % 2))
            layernorm(xb, C_ING, C_INB, tt)

    def phase_rnn(l, seq):
        cb = l * CPL
        S.add("pool", lambda e: e.memset(RXP.t[:, 0:3], 0.0), writes=RXP.keys(0, 3))
        sg = wq.get(l, SL_GATES)
        for c in range(8):
            s = wq.get(l, SL_RXY + c)
            wsk = WS[s].keys()
            for hf in range(2):
                u = 2 * c + hf
                t0 = hf * 1024
                xc, rr, xcb, ii, aa, hs = XC[u % 2], RR[u % 2], XCB[u % 2], II[u % 2], AA[u % 2], HS[u % 2]
                hs_prev = HS[(u + 1) % 2]
                for t2 in range(2):
                    tt = 2 * hf + t2
                    p = PS.alloc()
                    S.add("pe", mm_group(p.t[:, :], [(WS[s].t[:, kc * 256: kc * 256 + 128], HI.t[:, kc, tt * TT:(tt + 1) * TT])
                                                     for kc in range(8)]), reads=wsk + hik(tt), writes=[p.key])
                    S.add("act", lambda e, p=p, tt=tt: e.activation(out=RXP.t[:, 3 + tt * TT: 3 + (tt + 1) * TT], in_=p.t[:, :],
                                                                   func=AF.Copy),
                          reads=[p.key], writes=RXP.keys(3 + tt * TT, 3 + (tt + 1) * TT))
                    PS.free(p)
                S.add("pool", lambda e, c=c, xc=xc, t0=t0: e.tensor_scalar(
                    out=xc.t[:, :], in0=RXP.t[:, 3 + t0:3 + t0 + 1024], scalar1=cc(cb + C_CONVW + 24 + c),
                    scalar2=cc(cb + C_CONVB + c), op0=ALU.mult, op1=ALU.add),
                    reads=RXP.keys(3 + t0, 3 + t0 + 1024) + KCST, writes=xc.keys())
                for k in range(3):
                    S.add("dve", lambda e, c=c, k=k, xc=xc, t0=t0: e.scalar_tensor_tensor(
                        out=xc.t[:, :], in0=RXP.t[:, k + t0:k + t0 + 1024], scalar=cc(cb + C_CONVW + 8 * k + c), in1=xc.t[:, :],
                        op0=ALU.mult, op1=ALU.add),
                        reads=RXP.keys(k + t0, k + t0 + 1024) + xc.keys() + KCST, writes=xc.keys())
                S.add("dve", lambda e, xc=xc, xcb=xcb: e.tensor_copy(out=xcb.t[:, :], in_=xc.t[:, :]),
                      reads=xc.keys(), writes=xcb.keys())
                for which, dst, bcol in ((0, rr, C_BA), (1, ii, C_BX)):
                    for t2 in range(2):
                        p = PS.alloc()
                        S.add("pe", mm_group(p.t[:, :], [(WS[sg].t[:, c * 256 + which * 128: c * 256 + which * 128 + 128],
                                                          xcb.t[:, t2 * TT:(t2 + 1) * TT])]),
                              reads=WS[sg].keys() + xcb.keys(t2 * TT, (t2 + 1) * TT), writes=[p.key])
                        S.add("act", lambda e, p=p, t2=t2, dst=dst, bcol=bcol, c=c: e.activation(
                            out=dst.t[:, t2 * TT:(t2 + 1) * TT], in_=p.t[:, :], func=AF.Sigmoid, bias=cc(cb + bcol + c)),
                            reads=[p.key] + KCST, writes=dst.keys(t2 * TT, (t2 + 1) * TT))
                        PS.free(p)
                S.add("act", lambda e, c=c, aa=aa, rr=rr: e.activation(out=aa.t[:, :], in_=rr.t[:, :], func=AF.Exp,
                                                                      scale=DER.t[:, l * 16 + c: l * 16 + c + 1]),
                      reads=rr.keys() + KDER, writes=aa.keys())
                S.add("dve", lambda e, aa=aa, rr=rr: e.tensor_tensor(out=rr.t[:, :], in0=aa.t[:, :], in1=aa.t[:, :], op=ALU.mult),
                      reads=aa.keys(), writes=rr.keys())
                S.add("act", lambda e, rr=rr: e.activation(out=rr.t[:, :], in_=rr.t[:, :], func=AF.Ln, scale=-1.0, bias=1.0),
                      reads=rr.keys(), writes=rr.keys())
                S.add("act", lambda e, rr=rr: e.activation(out=rr.t[:, :], in_=rr.t[:, :], func=AF.Exp, scale=0.5),
                      reads=rr.keys(), writes=rr.keys())
                S.add("pool", lambda e, ii=ii, rr=rr: e.tensor_tensor(out=ii.t[:, :], in0=ii.t[:, :], in1=rr.t[:, :], op=ALU.mult),
                      reads=ii.keys() + rr.keys(), writes=ii.keys())
                S.add("pool", lambda e, ii=ii, xc=xc: e.tensor_tensor(out=ii.t[:, :], in0=ii.t[:, :], in1=xc.t[:, :], op=ALU.mult),
                      reads=ii.keys() + xc.keys(), writes=ii.keys())
                if hf == 0:
                    S.add("dve", lambda e, hs=hs, aa=aa, ii=ii: e.tensor_tensor_scan(
                        out=hs.t[:, :], data0=aa.t[:, :], data1=ii.t[:, :], initial=0.0, op0=ALU.mult, op1=ALU.add),
                        reads=aa.keys() + ii.keys(), writes=hs.keys())
                else:
                    S.add("dve", lambda e, hs=hs, aa=aa, ii=ii, hp=hs_prev: e.tensor_tensor_scan(
                        out=hs.t[:, :], data0=aa.t[:, :], data1=ii.t[:, :], initial=hp.t[:, 1023:1024], op0=ALU.mult, op1=ALU.add),
                        reads=aa.keys() + ii.keys() + hs_prev.keys(1023, 1024), writes=hs.keys())
                for t2 in range(2):
                    tt = 2 * hf + t2
                    p = PS.alloc()
                    S.add("pe", mm_group(p.t[:, :], [(WS[s].t[:, kc * 256 + 128: kc * 256 + 256], HI.t[:, kc, tt * TT:(tt + 1) * TT])
                                                     for kc in range(8)]), reads=wsk + hik(tt), writes=[p.key])
                    tG = TMP.alloc()
                    S.add("act", lambda e, p=p, tG=tG: e.activation(out=tG.t[:, :], in_=p.t[:, :], func=AF.Gelu_apprx_tanh),
                          reads=[p.key], writes=tG.keys())
                    S.add("dve", lambda e, tG=tG, hs=hs, c=c, tt=tt, t2=t2: e.tensor_tensor(
                        out=RNN.t[:, c, tt * TT:(tt + 1) * TT], in0=hs.t[:, t2 * TT:(t2 + 1) * TT], in1=tG.t[:, :], op=ALU.mult),
                        reads=hs.keys(t2 * TT, (t2 + 1) * TT) + tG.keys(), writes=RNN.keys3([c], tt * TT, (tt + 1) * TT))
                    PS.free(p)
                    TMP.free(tG)
            wq.done(s)
        wq.done(sg)

    def phase_attn(l, seq):
        sf = wq.get(l, SL_WF)
        pf = PS.alloc()

        def fproj(e):
            ins = None
            for tb in range(16):
                for kc in range(8):
                    ins = e.matmul(pf.t[:, tb * 8:(tb + 1) * 8], lhsT=HI.t[:, kc, tb * 128:(tb + 1) * 128],
                                   rhs=WS[sf].t[:, kc * 8:(kc + 1) * 8], start=(kc == 0), stop=(kc == 7))
            return ins
        S.add("pe", fproj, reads=HI.keys() + WS[sf].keys(), writes=[pf.key])
        wq.done(sf)
        S.add("dve", lambda e: e.tensor_tensor(
            out=XF.t[:, :, :], in0=pf.t[:, 0:128].rearrange("p (a b) -> p a b", b=8),
            in1=CST.t[:, None, C_BF + 8 * l: C_BF + 8 * l + 8].broadcast_to([128, 16, 8]), op=ALU.add),
            reads=[pf.key] + KCST, writes=XF.keys())
        PS.free(pf)
        S.add("act", lambda e: e.activation(out=XF.t[:, :, :], in_=XF.t[:, :, :], func=AF.Exp, scale=-1.0),
              reads=XF.keys(), writes=XF.keys())
        S.add("act", lambda e: e.activation(out=SPB.t[:, :, :], in_=XF.t[:, :, :], func=AF.Ln, bias=1.0),
              reads=XF.keys(), writes=SPB.keys())
        spflat = SPB.t[:, :, :].rearrange("p a b -> p (a b)")
        S.add("dve", lambda e: e.tensor_copy(out=SP3[0].t[:, :], in_=spflat), reads=SPB.keys(), writes=SP3[0].keys())
        S.add("dve", lambda e: e.tensor_tensor(out=R1B.t[:, :], in0=spflat, in1=SP3[0].t[:, :], op=ALU.subtract),
              reads=SPB.keys() + SP3[0].keys(), writes=R1B.keys())
        S.add("dve", lambda e: e.tensor_copy(out=SP3[1].t[:, :], in_=R1B.t[:, :]), reads=R1B.keys(), writes=SP3[1].keys())
        S.add("dve", lambda e: e.tensor_tensor(out=R1B.t[:, :], in0=R1B.t[:, :], in1=SP3[1].t[:, :], op=ALU.subtract),
              reads=R1B.keys() + SP3[1].keys(), writes=R1B.keys())
        S.add("dve", lambda e: e.tensor_copy(out=SP3[2].t[:, :], in_=R1B.t[:, :]), reads=R1B.keys(), writes=SP3[2].keys())
        sp3k = SP3[0].keys() + SP3[1].keys() + SP3[2].keys()
        pc = PS.alloc()
        pS = PS.alloc()
        S.add("pe", mm_group(pc.t[:, 0:128], [(tri_bf, SP3[i].t[:, :]) for i in range(3)]), reads=sp3k + KM16, writes=[pc.key])
        S.add("pe", mm_group(pS.t[:, 0:128], [(ones_bf, SP3[i].t[:, :]) for i in range(3)]), reads=sp3k + KM16, writes=[pS.key])
        S.add("dve", lambda e: e.tensor_copy(out=SBK.t[:, :, :], in_=pS.t[:, 0:128].rearrange("p (a b) -> p a b", b=8)),
              reads=[pS.key], writes=SBK.keys())

        def scans(e):
            ins = None
            for hd in range(8):
                ins = e.tensor_tensor_scan(out=INC.t[:, :, hd], data0=ones32, data1=SBK.t[:, :, hd], initial=0.0,
                                           op0=ALU.mult, op1=ALU.add)
            return ins
        S.add("dve", scans, reads=SBK.keys() + KM32, writes=INC.keys())
        S.add("dve", lambda e: e.tensor_tensor(out=EXB.t[:, :, :], in0=INC.t[:, :, :], in1=SBK.t[:, :, :], op=ALU.subtract),
              reads=INC.keys() + SBK.keys(), writes=EXB.keys())
        S.add("dve", lambda e: e.tensor_tensor(out=CSB.t[:, :, :], in0=pc.t[:, 0:128].rearrange("p (a b) -> p a b", b=8),
                                               in1=EXB.t[:, :, :], op=ALU.add),
              reads=[pc.key] + EXB.keys(), writes=CSB.keys())
        pm = PS.alloc()
        S.add("pe", mm_group(pm.t[:, 0:128], [(half_bf, SP3[i].t[:, :]) for i in range(3)]), reads=sp3k + KM16, writes=[pm.key])
        S.add("dve", lambda e: e.tensor_tensor(out=XF.t[:, :, :], in0=pm.t[:, 0:128].rearrange("p (a b) -> p a b", b=8),
                                               in1=EXB.t[:, :, :], op=ALU.add),
              reads=[pm.key] + EXB.keys(), writes=XF.keys())
        PS.free(pc, pS, pm)

        for hd in range(8):
            g = hd // 2
            vb = VB[g % 2]
            if hd % 2 == 0:
                sv = wq.get(l, SL_V + g)
                for t2 in range(8):
                    p = PS.alloc()

                    def vproj(e, p=p, t2=t2, sv=sv):
                        ins = None
                        for hh in range(2):
                            tb = 2 * t2 + hh
                            for kc in range(8):
                                ins = e.matmul(p.t[:, hh * 256:(hh + 1) * 256], lhsT=HI.t[:, kc, tb * 128:(tb + 1) * 128],
                                               rhs=WS[sv].t[:, kc * 256:(kc + 1) * 256], start=(kc == 0), stop=(kc == 7))
                        return ins
                    S.add("pe", vproj, reads=WS[sv].keys() + hik(t2 // 2), writes=[p.key])
                    S.add("dve", lambda e, p=p, t2=t2, vb=vb: e.tensor_copy(
                        out=vb.t[:, 2 * t2:2 * t2 + 2, :].rearrange("p a b -> p (a b)"), in_=p.t[:, :]),
                        reads=[p.key], writes=vb.keys(2 * t2 * 256, (2 * t2 + 2) * 256))
                    PS.free(p)
                wq.done(sv)
            sq = wq.get(l, SL_QK + hd)
            qt, kt, bt = QT[hd % 2], KT[hd % 2], BT[hd % 2]
            for tt in range(NTT):
                p = PS.alloc()
                S.add("pe", mm_group(p.t[:, :], [(WS[sq].t[:, kc * 256: kc * 256 + 128], HI.t[:, kc, tt * TT:(tt + 1) * TT])
                                                 for kc in range(8)]), reads=WS[sq].keys() + hik(tt), writes=[p.key])
                S.add("act", lambda e, p=p, tt=tt, qt=qt: e.activation(out=qt.t[:, tt * TT:(tt + 1) * TT], in_=p.t[:, :],
                                                                      func=AF.Copy, scale=1.0 / math.sqrt(128.0)),
                      reads=[p.key], writes=qt.keys(tt * TT, (tt + 1) * TT))
                PS.free(p)
                p = PS.alloc()
                S.add("pe", mm_group(p.t[:, :], [(WS[sq].t[:, kc * 256 + 128: kc * 256 + 256], HI.t[:, kc, tt * TT:(tt + 1) * TT])
                                                 for kc in range(8)]), reads=WS[sq].keys() + hik(tt), writes=[p.key])
                S.add("dve", lambda e, p=p, tt=tt, kt=kt: e.tensor_copy(out=kt.t[:, tt * TT:(tt + 1) * TT], in_=p.t[:, :]),
                      reads=[p.key], writes=kt.keys(tt * TT, (tt + 1) * TT))
                PS.free(p)
            wq.done(sq)

            def mkbt(e, hd=hd, bt=bt):
                ins = None
                for j in range(16):
                    ins = e.tensor_scalar(out=bt.t[:, 0:j + 1, j], in0=CSB.t[:, 0:j + 1, hd], scalar1=XF.t[:, j, hd:hd + 1],
                                          scalar2=None, op0=ALU.subtract)
                return ins
            S.add("pool", mkbt, reads=CSB.keys() + XF.keys(), writes=bt.keys())

            steps = [(Q, i) for Q in range(4) for i in range(4 * Q + 4)]
            state = {}
            pslot = [0]

            def emit_S(n):
                Q, i = steps[n]
                r = max(0, i - 4 * Q)
                ps = PS.alloc()
                sl = PSL[pslot[0] % 4]
                pslot[0] += 1

                def fS(e, ps=ps, Q=Q, i=i, r=r, kt=kt, qt=qt):
                    diag = i >= 4 * Q
                    ins = e.matmul(ps.t[:, r * 128:512], lhsT=kt.t[:, i * 128:(i + 1) * 128],
                                   rhs=qt.t[:, Q * TT + r * 128:(Q + 1) * TT], start=True, stop=not diag)
                    if diag:
                        ins = e.matmul(ps.t[:, r * 128:(r + 1) * 128], lhsT=ident_bf, rhs=maskneg_bf, start=False, stop=True)
                    return ins
                S.add("pe", fS, reads=kt.keys(i * 128, (i + 1) * 128) + qt.keys(Q * TT + r * 128, (Q + 1) * TT) + KM16,
                      writes=[ps.key])

                def fE(e, ps=ps, sl=sl, Q=Q, i=i, r=r, bt=bt):
                    ins = None
                    for jb in range(r, 4):
                        ins = e.activation(out=sl.t[:, jb * 128:(jb + 1) * 128], in_=ps.t[:, jb * 128:(jb + 1) * 128],
                                           func=AF.Exp, bias=bt.t[:, i, 4 * Q + jb:4 * Q + jb + 1])
                    return ins
                S.add("act", fE, reads=[ps.key] + bt.keys(), writes=sl.keys())
                PS.free(ps)
                state[n] = sl

            def emit_PV(n):
                Q, i = steps[n]
                r = max(0, i - 4 * Q)
                sl = state.pop(n)
                if i == 0:
                    state["O"] = PS.alloc()
                    state["R"] = PS.alloc()
                Ob, Rb = state["O"], state["R"]
                last = (i == 4 * Q + 3)

                def fPV(e, sl=sl, Ob=Ob, Rb=Rb, i=i, r=r, last=last, vb=vb, hd=hd):
                    e.matmul(Ob.t[:, r * 128:512], lhsT=vb.t[:, i, (hd % 2) * 128:(hd % 2) * 128 + 128],
                             rhs=sl.t[:, r * 128:512], start=(i == 0), stop=last)
                    return e.matmul(Rb.t[:, r * 128:512], lhsT=ones_bf, rhs=sl.t[:, r * 128:512], start=(i == 0), stop=last)
                S.add("pe", fPV, reads=sl.keys() + vb.keys(i * 256, (i + 1) * 256) + KM16, writes=[Ob.key, Rb.key])
                if last:
                    tR = TMP.alloc()
                    S.add("act", lambda e, tR=tR, Rb=Rb: e.activation(out=tR.t[:, :], in_=Rb.t[:, :], func=AF.Ln),
                          reads=[Rb.key], writes=tR.keys())
                    S.add("act", lambda e, tR=tR: e.activation(out=tR.t[:, :], in_=tR.t[:, :], func=AF.Exp, scale=-1.0),
                          reads=tR.keys(), writes=tR.keys())
                    S.add("dve", lambda e, tR=tR, Ob=Ob, Q=Q, hd=hd: e.tensor_tensor(out=ATT.t[:, hd, Q * TT:(Q + 1) * TT], in0=Ob.t[:, :],
                                                                             in1=tR.t[:, :], op=ALU.mult),
                          reads=[Ob.key] + tR.keys(), writes=ATT.keys3([hd], Q * TT, (Q + 1) * TT))
                    TMP.free(tR)
                    PS.free(Ob, Rb)

            SK = 2
            for n in range(len(steps) + SK):
                if n < len(steps):
                    emit_S(n)
                if n - SK >= 0:
                    emit_PV(n - SK)

    def phase_merge(l, seq):
        cb = l * CPL
        for s in range(4):
            sl4 = [wq.get(l, b + s) for b in (SL_GA, SL_GB, SL_BRA, SL_BRB)]
            srcs = [HI, HI, ATT, RNN]
            for tt in range(NTT):
                for c2 in range(2):
                    c = 2 * s + c2
                    pp = []
                    for w in range(4):
                        p = PS.alloc()
                        src = srcs[w]
                        S.add("pe", mm_group(p.t[:, :], [(WS[sl4[w]].t[:, kc * 256 + c2 * 128: kc * 256 + c2 * 128 + 128],
                                                          src.t[:, kc, tt * TT:(tt + 1) * TT]) for kc in range(8)]),
                              reads=WS[sl4[w]].keys() + src.keys3(range(8), tt * TT, (tt + 1) * TT), writes=[p.key])
                        pp.append(p)
                    tA = TMP.alloc()
                    tB = TMP.alloc()
                    S.add("act", lambda e, p=pp[0], tA=tA, c=c: e.activation(out=tA.t[:, :], in_=p.t[:, :], func=AF.Sigmoid,
                                                                            bias=cc(cb + C_BM0 + c)),
                          reads=[pp[0].key] + KCST, writes=tA.keys())
                    S.add("act", lambda e, p=pp[1], tB=tB, c=c: e.activation(out=tB.t[:, :], in_=p.t[:, :], func=AF.Sigmoid,
                                                                            bias=cc(cb + C_BM1 + c)),
                          reads=[pp[1].key] + KCST, writes=tB.keys())
                    S.add("dve", lambda e, p=pp[2], tA=tA: e.scalar_tensor_tensor(out=tA.t[:, :], in0=tA.t[:, :], scalar=INV_A,
                                                                                 in1=p.t[:, :], op0=ALU.mult, op1=ALU.mult),
                          reads=[pp[2].key] + tA.keys(), writes=tA.keys())
                    S.add("dve", lambda e, p=pp[3], tB=tB: e.scalar_tensor_tensor(out=tB.t[:, :], in0=tB.t[:, :], scalar=INV_A,
                                                                                 in1=p.t[:, :], op0=ALU.mult, op1=ALU.mult),
                          reads=[pp[3].key] + tB.keys(), writes=tB.keys())
                    S.add("pool", lambda e, tA=tA, tB=tB, c=c, tt=tt: e.tensor_tensor(
                        out=MG.t[:, c, tt * TT:(tt + 1) * TT], in0=tA.t[:, :], in1=tB.t[:, :], op=ALU.add),
                        reads=tA.keys() + tB.keys(), writes=MG.keys3([c], tt * TT, (tt + 1) * TT))
                    PS.free(*pp)
                    TMP.free(tA, tB)
            wq.done(*sl4)
        so = [wq.get(l, SL_WOUT + s) for s in range(4)]
        for tt in range(NTT):
            xb = X1A[tt % 2]
            for c in range(8):
                p = PS.alloc()
                ws = WS[so[c // 2]]
                S.add("pe", mm_group(p.t[:, :], [(ws.t[:, kc * 256 + (c % 2) * 128: kc * 256 + (c % 2) * 128 + 128],
                                                  MG.t[:, kc, tt * TT:(tt + 1) * TT]) for kc in range(8)] + res_pairs(c, tt)),
                      reads=ws.keys() + MG.keys3(range(8), tt * TT, (tt + 1) * TT) + res_keys(c, tt), writes=[p.key])
                S.add("act", lambda e, p=p, xb=xb, c=c: e.activation(out=xb.t[:, c, :], in_=p.t[:, :], func=AF.Copy),
                      reads=[p.key], writes=xb.keys3([c], 0, TT))
                PS.free(p)
            layernorm(xb, cb + C_MIXG, cb + C_MIXB, tt, eps=RES_EPS)
        wq.done(*so)

    def phase_ffn(l, seq):
        cb = l * CPL
        for half in range(2):
            for s in range(11):
                shg = wq.get(l, SL_HG + s)
                shu = wq.get(l, SL_HU + s)
                for t2 in range(2):
                    tt = half * 2 + t2
                    for c2 in range(2):
                        j = 2 * s + c2
                        pg = PS.alloc()
                        pu = PS.alloc()
                        for p, sl in ((pg, shg), (pu, shu)):
                            S.add("pe", mm_group(p.t[:, :], [(WS[sl].t[:, kc * 256 + c2 * 128: kc * 256 + c2 * 128 + 128],
                                                              HI.t[:, kc, tt * TT:(tt + 1) * TT]) for kc in range(8)]),
                                  reads=WS[sl].keys() + hik(tt), writes=[p.key])
                        tA = TMP.alloc()
                        S.add("act", lambda e, pg=pg, tA=tA: e.activation(out=tA.t[:, :], in_=pg.t[:, :], func=AF.Silu),
                              reads=[pg.key], writes=tA.keys())
                        S.add("dve", lambda e, pu=pu, tA=tA, j=j, t2=t2: e.scalar_tensor_tensor(
                            out=ACTT.t[:, j, t2 * TT:(t2 + 1) * TT], in0=tA.t[:, :], scalar=INV_A, in1=pu.t[:, :],
                            op0=ALU.mult, op1=ALU.mult),
                            reads=[pu.key] + tA.keys(), writes=ACTT.keys3([j], t2 * TT, (t2 + 1) * TT))
                        PS.free(pg, pu)
                        TMP.free(tA)
                wq.done(shg, shu)
            for c in range(8):
                s0 = wq.get(l, SL_FFO + 2 * c)
                s1 = wq.get(l, SL_FFO + 2 * c + 1)
                for t2 in range(2):
                    tt = half * 2 + t2
                    p = PS.alloc()
                    pairs = []
                    for j in range(NFF):
                        ws = WS[s0] if j < 11 else WS[s1]
                        pairs.append((ws.t[:, (j % 11) * 128:(j % 11) * 128 + 128], ACTT.t[:, j, t2 * TT:(t2 + 1) * TT]))
                    S.add("pe", mm_group(p.t[:, :], pairs + res_pairs(c, tt)),
                          reads=WS[s0].keys() + WS[s1].keys() + ACTT.keys3(range(NFF), t2 * TT, (t2 + 1) * TT) + res_keys(c, tt),
                          writes=[p.key])
                    S.add("act", lambda e, p=p, xb=X1W[t2], c=c: e.activation(out=xb.t[:, c, :], in_=p.t[:, :], func=AF.Copy),
                          reads=[p.key], writes=X1W[t2].keys3([c], 0, TT))
                    PS.free(p)
                wq.done(s0, s1)
            for t2 in range(2):
                layernorm(X1W[t2], cb + C_FFNG, cb + C_FFNB, half * 2 + t2, eps=RES_EPS)

    def phase_ple(l, seq, last):
        cb = l * CPL
        S.add("pool", lambda e: e.dma_start(out=PTB.t[:, :, :], in_=pT[l, seq].rearrange("(kc p) t -> p kc t", p=128)),
              writes=PTB.keys(), dma=("p", 0))
        sp4 = [wq.get(l, SL_WPG + s) for s in range(4)]
        sw = wq.get(l, SL_WPLE)
        for tt in range(NTT):
            xb = X1W[tt % 2]
            for c in range(8):
                pg = PS.alloc()
                pe_ = PS.alloc()
                ws = WS[sp4[c // 2]]
                S.add("pe", mm_group(pg.t[:, :], [(ws.t[:, kc * 256 + (c % 2) * 128: kc * 256 + (c % 2) * 128 + 128],
                                                   HI.t[:, kc, tt * TT:(tt + 1) * TT]) for kc in range(8)]),
                      reads=ws.keys() + hik(tt), writes=[pg.key])
                S.add("pe", mm_group(pe_.t[:, :], [(WS[sw].t[:, kc * 1024 + c * 128: kc * 1024 + c * 128 + 128],
                                                    PTB.t[:, kc, tt * TT:(tt + 1) * TT]) for kc in range(2)]),
                      reads=WS[sw].keys() + PTB.keys3(range(2), tt * TT, (tt + 1) * TT), writes=[pe_.key])
                tA = TMP.alloc()
                S.add("act", lambda e, pg=pg, tA=tA, c=c: e.activation(out=tA.t[:, :], in_=pg.t[:, :], func=AF.Sigmoid,
                                                                      bias=cc(cb + C_BPG + c)),
                      reads=[pg.key] + KCST, writes=tA.keys())
                S.add("dve", lambda e, pe_=pe_, tA=tA: e.scalar_tensor_tensor(out=tA.t[:, :], in0=tA.t[:, :], scalar=INV_A,
                                                                             in1=pe_.t[:, :], op0=ALU.mult, op1=ALU.mult),
                      reads=[pe_.key] + tA.keys(), writes=tA.keys())
                pr = PS.alloc()
                S.add("pe", mm_group(pr.t[:, :], res_pairs(c, tt)), reads=res_keys(c, tt), writes=[pr.key])
                S.add("dve", lambda e, pr=pr, tA=tA, xb=xb, c=c: e.tensor_tensor(out=xb.t[:, c, :], in0=tA.t[:, :], in1=pr.t[:, :],
                                                                                op=ALU.add),
                      reads=[pr.key] + tA.keys(), writes=xb.keys3([c], 0, TT))
                PS.free(pg, pe_, pr)
                TMP.free(tA)
            layernorm(xb, cb + C_PLEG, cb + C_PLEB, tt, final=last, seq=seq, eps=RES_EPS)
        wq.done(*sp4)
        wq.done(sw)

    for seq in range(n_seq):
        phase_entry(seq)
        for l in range(n_layers):
            phase_rnn(l, seq)
            phase_attn(l, seq)
            phase_merge(l, seq)
            phase_ffn(l, seq)
            phase_ple(l, seq, last=(l == n_layers - 1))
    assert wq.cons == len(wq.seq), (wq.cons, len(wq.seq))

    outk = [("out", s, tt) for s in range(n_seq) for tt in range(NTT)]

    def fin(e):
        return e.nop()
    S.add("sp", fin, reads=outk, writes=[("fin",)])
    S.emit(nc, reorder=REORDER)
    return nc


_CACHE = {}


def kernel(**inputs):
    inp = {k: np.asarray(v) for k, v in inputs.items()}
    x = inp["x"].astype(np.float32, copy=False)
    p = inp["p"].astype(np.float32, copy=False)
    B = x.shape[0]
    nseq = B // NCORES
    W = _build_slabs(inp)
    C = _build_consts(inp)
    m16, m32 = _build_masks()
    xT = np.ascontiguousarray(x.transpose(0, 2, 1))
    pTr = np.ascontiguousarray(p.transpose(0, 1, 3, 2))
    if "nc" not in _CACHE:
        _CACHE["nc"] = build_program(L, nseq)
    nc = _CACHE["nc"]
    in_maps = []
    for c in range(NCORES):
        in_maps.append({
            "xT": xT[c * nseq:(c + 1) * nseq],
            "pT": np.ascontiguousarray(pTr[:, c * nseq:(c + 1) * nseq]),
            "W": W, "cst": C, "m16": m16, "m32": m32,
        })
    res = run_bass_kernel_spmd(nc, in_maps, core_ids=list(range(NCORES)))
    outs = [res.results[c]["outT"] for c in range(NCORES)]
    oT = np.concatenate(outs, axis=0)
    return np.ascontiguousarray(oT.transpose(0, 2, 1)).astype(np.float32, copy=False)
```
